# Optimizing a Trainium2 kernel written in Bass

```python
import math
import jax, jax.numpy as jnp
from jax import lax
import numpy as np


D_MODEL = 2048
BATCH = 1
SEQ = 16384
DEPTH = 2

N_MEM = 256
CONV_WIDTH = D_MODEL // 2
CONV_K = 3
HG_HEADS = 8
HG_KEY_DIM = 128
HG_VAL_DIM = D_MODEL // 2 // HG_HEADS
HG_FWIDTH = HG_HEADS * HG_KEY_DIM
HG_VWIDTH = HG_HEADS * HG_VAL_DIM
HG_CHUNK = 64
LOG_TINY = 1e-30
MEM_HEADS = 4
MEM_HEAD_DIM = D_MODEL // 2 // MEM_HEADS
MEM_WIDTH = MEM_HEADS * MEM_HEAD_DIM
BRANCH_WIDTH = D_MODEL // 2
N_BRANCH = 3
FFN_DIM = 256 * (-(-8 * D_MODEL // (3 * 256)))
RMS_EPS = 1e-6
IN_SPLITS = [CONV_WIDTH, CONV_WIDTH, CONV_WIDTH,
             HG_FWIDTH, HG_FWIDTH, HG_VWIDTH, HG_VWIDTH,
             MEM_WIDTH,
             D_MODEL, D_MODEL, D_MODEL]
N_IN = sum(IN_SPLITS)

kernel_name = 'hybrid_conv_hgrn2_memxattn_block'


def rms_norm(x, w):
    xf = x.astype(jnp.float32)
    y = xf * lax.rsqrt(jnp.mean(xf * xf, axis=-1, keepdims=True) + RMS_EPS)
    return (y * w.astype(jnp.float32)).astype(x.dtype)


def causal_dwconv3(u, w):
    s = u.shape[1]
    up = jnp.pad(u, ((0, 0), (CONV_K - 1, 0), (0, 0)))
    return up[:, :s] * w[0] + up[:, 1:s + 1] * w[1] + up[:, 2:] * w[2]


def hgrn2_chunked(q, k, v, log_f):
    bsz, s, h, dk = q.shape
    dv = v.shape[-1]
    n = s // HG_CHUNK

    def chunks(a):
        a = a.astype(jnp.float32).reshape(bsz, n, HG_CHUNK, h, a.shape[-1])
        return jnp.moveaxis(a, 1, 0)

    qc, kc, vc = chunks(q), chunks(k), chunks(v)
    bc = jnp.cumsum(chunks(log_f), axis=2)
    mask = jnp.tril(jnp.ones((HG_CHUNK, HG_CHUNK), dtype=bool))[None, :, :, None, None]

    def step(state, inp):
        qi, ki, vi, bi = inp
        o_inter = jnp.einsum('bchk,bhkv->bchv', qi * jnp.exp(bi), state)
        diff = bi[:, :, None] - bi[:, None, :]
        decay = jnp.where(mask, jnp.exp(jnp.minimum(diff, 0.0)), 0.0)
        scores = jnp.einsum('bihk,bijhk,bjhk->bhij', qi, decay, ki)
        o_intra = jnp.einsum('bhij,bjhv->bihv', scores, vi)
        b_last = bi[:, -1]
        k_to_end = ki * jnp.exp(b_last[:, None] - bi)
        state = jnp.exp(b_last)[..., None] * state + jnp.einsum('bjhk,bjhv->bhkv', k_to_end, vi)
        return state, o_inter + o_intra

    s0 = jnp.zeros((bsz, h, dk, dv), jnp.float32)
    _, o = lax.scan(step, s0, (qc, kc, vc, bc))
    return jnp.moveaxis(o, 0, 1).reshape(bsz, s, h, dv)


def setup_inputs(seed: int = 0) -> dict:
    key = jax.random.key(seed)
    ks = jax.random.split(key, 20)

    def nrm(k, shape, scale):
        return jax.random.normal(k, shape, jnp.float32) * scale

    def gain(k):
        return 1.0 + nrm(k, (DEPTH, D_MODEL), 0.02)

    return {
        'x': nrm(ks[0], (BATCH, SEQ, D_MODEL), 1.0),
        'mem': nrm(ks[1], (BATCH, N_MEM, D_MODEL), 1.0),
        'w_in': nrm(ks[2], (DEPTH, D_MODEL, N_IN), D_MODEL ** -0.5),
        'conv_mix_w': nrm(ks[3], (DEPTH, CONV_K, CONV_WIDTH), CONV_K ** -0.5),
        'hg_lower_bounds': nrm(ks[4], (DEPTH, HG_FWIDTH), 0.1),
        'hg_norm_w': 1.0 + nrm(ks[5], (DEPTH, HG_VWIDTH), 0.02),
        'w_mem_kv': nrm(ks[6], (DEPTH, D_MODEL, 2 * MEM_WIDTH), D_MODEL ** -0.5),
        'w_branch': nrm(ks[7], (DEPTH, N_BRANCH, BRANCH_WIDTH, D_MODEL), BRANCH_WIDTH ** -0.5),
        'w_out': nrm(ks[8], (DEPTH, D_MODEL, D_MODEL), D_MODEL ** -0.5),
        'norm_mix_pre': gain(ks[9]),
        'norm_mix_post': gain(ks[10]),
        'norm_mem': gain(ks[11]),
        'norm_ffn_pre': gain(ks[12]),
        'norm_ffn_post': gain(ks[13]),
        'w_ffn_up': nrm(ks[14], (DEPTH, D_MODEL, 2 * FFN_DIM), D_MODEL ** -0.5),
        'conv_ffn_w': nrm(ks[15], (DEPTH, CONV_K, FFN_DIM), CONV_K ** -0.5),
        'conv_ffn_b': nrm(ks[16], (DEPTH, FFN_DIM), 0.01),
        'w_ffn_down': nrm(ks[17], (DEPTH, FFN_DIM, D_MODEL), FFN_DIM ** -0.5),
    }


def reference(x, mem, w_in, conv_mix_w, hg_lower_bounds, hg_norm_w, w_mem_kv, w_branch, w_out,
              norm_mix_pre, norm_mix_post, norm_mem, norm_ffn_pre, norm_ffn_post,
              w_ffn_up, conv_ffn_w, conv_ffn_b, w_ffn_down):
    bsz, s, _ = x.shape
    dt = x.dtype
    lb_soft = jax.nn.softmax(hg_lower_bounds.astype(jnp.float32), axis=0)
    lb_all = jnp.cumsum(lb_soft, axis=0) - lb_soft[0]
    split_idx = list(np.cumsum(IN_SPLITS)[:-1])

    for l in range(DEPTH):
        h = rms_norm(x, norm_mix_pre[l])
        proj = jnp.einsum('bsd,dn->bsn', h, w_in[l])
        (cb, cc, cv, hq, hf, hi, hg, mq, ga, gb, gm) = jnp.split(proj, split_idx, axis=-1)

        y_a = cb * causal_dwconv3(cc * cv, conv_mix_w[l])

        q = jax.nn.silu(hq).reshape(bsz, s, HG_HEADS, HG_KEY_DIM) * (HG_KEY_DIM ** -0.5)
        lb = jnp.clip(lb_all[l], 0.0, 1.0).reshape(HG_HEADS, HG_KEY_DIM)
        zf = hf.astype(jnp.float32).reshape(bsz, s, HG_HEADS, HG_KEY_DIM)
        log_f = jnp.logaddexp(jnp.log(lb + LOG_TINY), jnp.log1p(-lb) + jax.nn.log_sigmoid(zf))
        k = -jnp.expm1(log_f)
        vin = hi.reshape(bsz, s, HG_HEADS, HG_VAL_DIM)
        o = hgrn2_chunked(q, k, vin, log_f)
        o = o * lax.rsqrt(jnp.mean(o * o, axis=-1, keepdims=True) + RMS_EPS)
        o = o * hg_norm_w[l].astype(jnp.float32).reshape(HG_HEADS, HG_VAL_DIM)
        y_b = (o.reshape(bsz, s, HG_VWIDTH) * jax.nn.silu(hg.astype(jnp.float32))).astype(dt)

        mh = rms_norm(mem, norm_mem[l])
        mk, mv = jnp.split(jnp.einsum('bmd,dn->bmn', mh, w_mem_kv[l]), 2, axis=-1)
        qh = mq.reshape(bsz, s, MEM_HEADS, MEM_HEAD_DIM)
        kh = mk.reshape(bsz, N_MEM, MEM_HEADS, MEM_HEAD_DIM)
        vh = mv.reshape(bsz, N_MEM, MEM_HEADS, MEM_HEAD_DIM)
        sc = jnp.einsum('bshd,bmhd->bhsm', qh, kh).astype(jnp.float32) * (MEM_HEAD_DIM ** -0.5)
        p = jax.nn.softmax(sc, axis=-1).astype(dt)
        y_m = jnp.einsum('bhsm,bmhd->bshd', p, vh).reshape(bsz, s, MEM_WIDTH)

        merged = (jax.nn.sigmoid(ga) * jnp.einsum('bsc,cd->bsd', y_a, w_branch[l, 0])
                  + jax.nn.sigmoid(gb) * jnp.einsum('bsc,cd->bsd', y_b, w_branch[l, 1])
                  + jax.nn.sigmoid(gm) * jnp.einsum('bsc,cd->bsd', y_m, w_branch[l, 2]))
        mix_out = jnp.einsum('bsd,de->bse', merged, w_out[l])
        x = x + rms_norm(mix_out, norm_mix_post[l])

        h2 = rms_norm(x, norm_ffn_pre[l])
        up_g, up_v = jnp.split(jnp.einsum('bsd,df->bsf', h2, w_ffn_up[l]), 2, axis=-1)
        act = jax.nn.gelu(causal_dwconv3(up_g, conv_ffn_w[l]) + conv_ffn_b[l], approximate=True) * up_v
        ffn_out = jnp.einsum('bsf,fd->bsd', act, w_ffn_down[l])
        x = x + rms_norm(ffn_out, norm_ffn_post[l])
    return x
```

```python
import numpy as np
import ml_dtypes
import concourse.bass as bass
import concourse.mybir as mybir
from concourse.bass_utils import run_bass_kernel_spmd

F32 = mybir.dt.float32
BF16 = mybir.dt.bfloat16
AF = mybir.ActivationFunctionType
ALU = mybir.AluOpType
AX = mybir.AxisListType

D = 2048
KC = 16
T = 2048
NCORES = 8
FFD = 5632
FC = 44
NIN = 14336
EPS = 1e-6


class Buf:
    __slots__ = ("name", "lw", "rd")

    def __init__(self, name=""):
        self.name = name
        self.lw = None
        self.rd = []


class Op:
    __slots__ = ("eng", "fn", "deps", "idx", "dma", "stream", "signal", "semval", "dcount")

    def __init__(self, eng, fn, idx, dma, stream):
        self.eng = eng
        self.fn = fn
        self.idx = idx
        self.deps = []
        self.dma = dma
        self.stream = stream
        self.signal = False
        self.semval = 0
        self.dcount = 0


class Prog:
    ENGS = ("pe", "act", "dve", "pool", "sp")

    def __init__(self, nc):
        self.nc = nc
        self.ops = []
        self.streams = {}
        self.outs = []
        self.last_dma = {}

    def op(self, eng, fn, reads=(), writes=(), dma=False, stream=None):
        o = Op(eng, fn, len(self.ops), dma, stream)
        deps = set()
        for b in reads:
            if b.lw is not None:
                deps.add(b.lw)
        for b in writes:
            if b.lw is not None:
                deps.add(b.lw)
            deps.update(b.rd)
        for b in reads:
            b.rd.append(o.idx)
        for b in writes:
            b.lw = o.idx
            b.rd = []
        deps.discard(o.idx)
        o.deps = sorted(deps)
        if dma:
            self.streams[stream] = self.streams.get(stream, 0) + 1
            o.dcount = 16 * self.streams[stream]
            self.last_dma[stream] = o.idx
        self.ops.append(o)
        return o

    def dma(self, eng, out, in_, reads, writes, stream, final=False):
        o = self.op(eng, lambda e: e.dma_start(out=out, in_=in_), reads, writes, dma=True, stream=stream)
        if final:
            self.outs.append(o.idx)
        return o

    def emit(self):
        nc = self.nc
        ops = self.ops
        fin = self.op("sp", lambda e: None, [], [])
        fin.deps = sorted(set(self.outs))
        for o in ops:
            for d in o.deps:
                y = ops[d]
                if y.dma:
                    continue
                if y.eng == o.eng and o.eng in ("pe", "sp"):
                    continue
                y.signal = True
        cnt = {e: 0 for e in self.ENGS}
        for o in ops:
            if o.signal and not o.dma:
                cnt[o.eng] += 1
                o.semval = cnt[o.eng]
        esem = {e: nc.alloc_semaphore("es_" + e) for e in self.ENGS}
        ssem = {s: nc.alloc_semaphore("ds_%d" % i) for i, s in enumerate(self.streams)}
        per = {e: [o for o in ops if o.eng == e] for e in self.ENGS}

        def run(engname, e):
            seen = {}
            for o in per[engname]:
                for d in o.deps:
                    y = ops[d]
                    if y.dma:
                        sem, val = ssem[y.stream], y.dcount
                    else:
                        if y.eng == engname and engname in ("pe", "sp"):
                            continue
                        sem, val = esem[y.eng], y.semval
                    k = sem.num
                    if seen.get(k, 0) >= val:
                        continue
                    seen[k] = val
                    e.wait_ge(sem, val)
                ins = o.fn(e)
                if ins is None:
                    continue
                if o.dma:
                    ins.then_inc(ssem[o.stream], 16)
                elif o.signal:
                    ins.then_inc(esem[engname], 1)

        with nc.Block() as block:
            @block.tensor
            def _(e):
                run("pe", e)

            @block.scalar
            def _(e):
                run("act", e)

            @block.vector
            def _(e):
                run("dve", e)

            @block.gpsimd
            def _(e):
                run("pool", e)

            @block.sync
            def _(e):
                run("sp", e)


class Cx:
    def __init__(self):
        self.nc = bass.Bass("TRN2", target_bir_lowering=False)
        self.P = Prog(self.nc)
        nc = self.nc
        self.ps = [nc.alloc_psum_tensor("ps%d" % i, [128, 512], F32) for i in range(7)]
        self.psb = [Buf("ps%d" % i) for i in range(7)]
        self.pst = nc.alloc_psum_tensor("pst", [128, 1024], BF16)
        self.pstb = Buf("pst")
        self.ones = nc.alloc_sbuf_tensor("ones_bf", [128, 128], BF16)
        self.onesb = Buf("ones")
        self.epsT = nc.alloc_sbuf_tensor("epsT", [128, 1], F32)
        self.epsb = Buf("eps")
        self.memset("dve", self.ones[:], 1.0, [self.onesb])
        self.memset("dve", self.epsT[:], EPS, [self.epsb])
        self.nsb = 0

    def arena_init(self, nbytes):
        self.arena = self.sb([128, nbytes // 2], BF16, "ARENA")
        self.aoff = 0
        self.abase = 0
        self.fs = {e: self.sb([128, 1], F32, "fs_" + e) for e in ("act", "dve", "pool")}

    def stage(self):
        P = self.P
        xs = []
        t = self.fs["act"]
        xs.append(P.op("act", lambda e, t=t: e.activation(out=t[:], in_=self.epsT[:], func=AF.Copy), [self.epsb], []).idx)
        for eng in ("dve", "pool"):
            t = self.fs[eng]
            xs.append(P.op(eng, lambda e, t=t: e.memset(t[:], 0.0), [], []).idx)
        deps = sorted(set(xs + list(P.last_dma.values())))
        for eng in Prog.ENGS:
            y = P.op(eng, lambda e: None, [], [])
            y.deps = list(deps)
        self.aoff = self.abase

    def av(self, shape, dt):
        n = 1
        for d in shape[1:]:
            n *= d
        nb = n * (2 if dt == BF16 else 4)
        nb = (nb + 63) // 64 * 64
        assert self.aoff + nb <= self.arena.shape[1] * 2, ("arena overflow", self.aoff, nb)
        v = self.arena[:, self.aoff // 2:(self.aoff + nb) // 2]
        self.aoff += nb
        if dt != BF16:
            v = v.bitcast(dt)
        v = v[:, 0:n]
        if len(shape) == 3:
            v = v.rearrange("p (a b) -> p a b", b=shape[2])
        return v

    def sb(self, shape, dt, name=None):
        self.nsb += 1
        return self.nc.alloc_sbuf_tensor("s_" + (name or ("t%d" % self.nsb)), list(shape), dt)

    def din(self, name, shape, dt=F32):
        return self.nc.dram_tensor(name, list(shape), dt, kind="ExternalInput").ap()

    def dout(self, name, shape, dt=F32):
        return self.nc.dram_tensor(name, list(shape), dt, kind="ExternalOutput").ap()

    def dscr(self, name, shape, dt=F32):
        return self.nc.dram_tensor(name, list(shape), dt).ap()

    def memset(self, eng, ap, val, wr):
        self.P.op(eng, lambda e: e.memset(ap, val), [], wr)

    def mm(self, out, lhsT, rhs, start, stop, rd, wr):
        self.P.op("pe", lambda e: e.matmul(out, lhsT, rhs, start=start, stop=stop), rd, wr)

    def tr(self, out, in_, ident, rd, wr):
        self.P.op("pe", lambda e: e.transpose(out, in_, ident), rd, wr)

    def act(self, out, in_, func, rd, wr, scale=1.0, bias=None, accum=None):
        def f(e):
            kw = {}
            if bias is not None:
                kw["bias"] = bias
            if accum is not None:
                kw["accum_out"] = accum
            return e.activation(out=out, in_=in_, func=func, scale=scale, **kw)
        self.P.op("act", f, rd, wr)

    def ts(self, eng, out, in0, s1, s2, op0, op1, rd, wr):
        if s2 is None:
            self.P.op(eng, lambda e: e.tensor_scalar(out=out, in0=in0, scalar1=s1, scalar2=None, op0=op0), rd, wr)
        else:
            self.P.op(eng, lambda e: e.tensor_scalar(out=out, in0=in0, scalar1=s1, scalar2=s2, op0=op0, op1=op1), rd, wr)

    def tt(self, eng, out, in0, in1, op, rd, wr):
        self.P.op(eng, lambda e: e.tensor_tensor(out=out, in0=in0, in1=in1, op=op), rd, wr)

    def stt(self, out, in0, scalar, in1, op0, op1, rd, wr):
        self.P.op("dve", lambda e: e.scalar_tensor_tensor(out=out, in0=in0, scalar=scalar, in1=in1, op0=op0, op1=op1), rd, wr)

    def copy(self, eng, out, in_, rd, wr):
        if eng == "act":
            self.act(out, in_, AF.Copy, rd, wr)
        else:
            self.P.op(eng, lambda e: e.tensor_copy(out=out, in_=in_), rd, wr)

    def recip(self, out, in_, rd, wr):
        self.P.op("dve", lambda e: e.reciprocal(out=out, in_=in_), rd, wr)

    def scan(self, out, d0, d1, rd, wr):
        self.P.op("dve", lambda e: e.tensor_tensor_scan(out=out, data0=d0, data1=d1, initial=0.0, op0=ALU.mult, op1=ALU.add), rd, wr)

    def dma(self, eng, out, in_, rd, wr, stream, final=False):
        return self.P.dma(eng, out, in_, rd, wr, stream, final)

    def load_const(self, dram_ap, shape, dt=F32, name=None):
        t = self.sb(shape, dt, name)
        b = Buf(name or "c")
        self.dma("sp", t[:], dram_ap, [], [b], "c_%s" % (name or str(self.nsb)))
        return t, b


class NormRes:
    def __init__(self, cx, ntw=256):
        self.ntw = ntw
        self.xt = cx.sb([128, KC, ntw], F32, "n_xt")
        self.xtb = [Buf("xt%d" % g) for g in range(4)]
        self.sq = [cx.sb([128, 4, ntw], BF16, "n_sq%d" % i) for i in range(2)]
        self.sqb = [Buf("sq%d" % i) for i in range(2)]
        self.rt = cx.sb([128, ntw], F32, "n_rt")
        self.rtb = Buf("rt")
        self.cnt = 0


def rstd_from_ss(cx, out, ss, w, dim, rd, wrb):
    cx.act(out, ss, AF.Sqrt, rd + [cx.epsb], [wrb], scale=1.0 / dim, bias=cx.epsT[:, 0:1])
    cx.recip(out, out, [wrb], [wrb])


def norm_stage(cx, nr, xT, nw, h, hbuf, ranges, ssbank=6):
    xv = xT.rearrange("(k p) t -> p k t", p=128)
    ss, ssb = cx.ps[ssbank], cx.psb[ssbank]
    for ri, (c0, w) in enumerate(ranges):
        x2 = nr.rcnt % 2
        nr.rcnt += 1
        xt, xtb = nr.xt[x2], nr.xtb[x2]
        for g in range(4):
            cx.dma("sp", xt[:, g * 4:(g + 1) * 4, :w], xv[:, g * 4:(g + 1) * 4, c0:c0 + w], [], [xtb[g]], "nxt%d_%d" % (x2, g))
            s = nr.cnt % 2
            nr.cnt += 1
            cx.act(nr.sq[s][:, :, :w], xt[:, g * 4:(g + 1) * 4, :w], AF.Square, [xtb[g]], [nr.sqb[s]])
            for j in range(4):
                cx.mm(ss[:, :w], cx.ones[:], nr.sq[s][:, j, :w], g == 0 and j == 0, g == 3 and j == 3,
                      [nr.sqb[s], cx.onesb], [ssb])
        r2 = x2
        rstd_from_ss(cx, nr.rt[r2][:, :w], ss[:, :w], w, D, [ssb], nr.rtb[r2])
        for k in range(KC):
            cx.stt(h[:, k, c0:c0 + w], xt[:, k, :w], nw[:, k:k + 1], nr.rt[r2][:, :w], ALU.mult, ALU.mult,
                   [xtb[k // 4], nr.rtb[r2]], [hbuf(k, c0)])


def make_nr(cx, alloc):
    class NR:
        pass
    nr = NR()
    nr.ntw = 256
    nr.xt = [alloc([128, KC, 256], F32) for _ in range(2)]
    nr.xtb = [[Buf() for g in range(4)] for _ in range(2)]
    nr.sq = [alloc([128, 4, 256], BF16) for i in range(2)]
    nr.sqb = [Buf() for i in range(2)]
    nr.rt = [alloc([128, 256], F32) for _ in range(2)]
    nr.rtb = [Buf(), Buf()]
    nr.cnt = 0
    nr.rcnt = 0
    return nr


def down_and_post(cx, kch, wsgen, rhs, rhs_rd, nw2, xT, out, alloc, load_half=None):
    MOf = alloc([128, 2 * KC * 512], F32)
    mobuf = [[Buf() for _ in range(KC)] for _ in range(2)]
    SQ = [alloc([128, 512], BF16) for _ in range(2)]
    sqb = [Buf(), Buf()]
    xi = [alloc([128, 512], F32) for _ in range(4)]
    xib = [Buf() for _ in range(4)]
    ot = [alloc([128, 512], F32) for _ in range(2)]
    otb = [Buf() for _ in range(2)]
    rt = [alloc([128, 512], F32) for _ in range(2)]
    rtb = [Buf(), Buf()]
    cnt = 0
    pc = 0
    for hf in range(2):
        if load_half is not None:
            load_half(hf)
        for n in range(KC):
            s = next(wsgen)
            for t in range(2):
                ti = hf * 2 + t
                b = (cnt % 2) * 2 + t
                for j in range(kch):
                    cx.mm(cx.ps[b][:], s.t[:, j, :], rhs(j, ti, t), j == 0, j == kch - 1, [s.b] + rhs_rd(j), [cx.psb[b]])
                q = (cnt * 2 + t) % 2
                mo = MOf[:, (t * KC + n) * 512:(t * KC + n + 1) * 512]
                cx.copy("act", mo, cx.ps[b][:], [cx.psb[b]], [mobuf[t][n]])
                cx.act(SQ[q], cx.ps[b][:], AF.Square, [cx.psb[b]], [sqb[q]])
                cx.mm(cx.ps[4 + t][:], cx.ones[:], SQ[q], n == 0, n == KC - 1, [sqb[q], cx.onesb], [cx.psb[4 + t]])
            cnt += 1
        chunks = [(t, n) for t in range(2) for n in range(KC)]

        def load_x(i):
            t, n = chunks[i]
            ti = hf * 2 + t
            s4 = (pc + i) % 4
            cx.dma("sp", xi[s4], xT[n * 128:(n + 1) * 128, 2 + ti * 512:2 + (ti + 1) * 512], [], [xib[s4]], "xi%d" % s4)

        for i in range(3):
            load_x(i)
        for t in range(2):
            rstd_from_ss(cx, rt[t], cx.ps[4 + t][:], 512, D, [cx.psb[4 + t]], rtb[t])
        for i, (t, n) in enumerate(chunks):
            ti = hf * 2 + t
            if i + 3 < len(chunks):
                load_x(i + 3)
            s4 = (pc + i) % 4
            o2 = (pc + i) % 2
            mo = MOf[:, (t * KC + n) * 512:(t * KC + n + 1) * 512]
            cx.stt(mo, mo, nw2[:, n:n + 1], rt[t], ALU.mult, ALU.mult, [mobuf[t][n], rtb[t]], [mobuf[t][n]])
            cx.tt("dve", ot[o2], mo, xi[s4], ALU.add, [mobuf[t][n], xib[s4]], [otb[o2]])
            cx.dma("sp", out[n * 128:(n + 1) * 128, ti * 512:(ti + 1) * 512], ot[o2], [otb[o2]], [], "ot%d" % o2, final=True)
        pc += len(chunks)


def halo_ranges(ntw):
    r = [(0, 2)]
    c = 2
    while c < T + 2:
        r.append((c, ntw))
        c += ntw
    return r


def plain_ranges(ntw, total=T):
    return [(c, ntw) for c in range(0, total, ntw)]


TILES_H = [(0, 2)] + [(2 + i * 512, 512) for i in range(4)]
TILES_P = [(i * 512, 512) for i in range(4)]


def tile_of_h(c0):
    return 0 if c0 < 2 else 1 + (c0 - 2) // 512


def post_norm_residual(cx, ss, ssb, moS, mob, nw2, xT_cols, outT_cols, tagp):
    mi, mib, xi, xib, ot, otb, nr_rt, nr_rtb = tagp
    rstd_from_ss(cx, nr_rt[:, :512], ss[:, :512], 512, D, [ssb], nr_rtb)
    for n in range(KC):
        s = n % 3
        cx.dma("sp", mi[s][:, :], moS[n], [mob[n]], [mib[s]], "mi%d" % s)
        cx.dma("sp", xi[s][:, :], xT_cols(n), [], [xib[s]], "xi%d" % s)
        cx.stt(mi[s][:, :], mi[s][:, :], nw2[:, n:n + 1], nr_rt[:, :512], ALU.mult, ALU.mult, [mib[s], nr_rtb], [mib[s]])
        cx.tt("dve", ot[s][:, :], mi[s][:, :], xi[s][:, :], ALU.add, [mib[s], xib[s]], [otb[s]])
        cx.dma("sp", outT_cols(n), ot[s][:, :], [otb[s]], [], "ot%d" % s, final=True)


def alloc_post(cx):
    mi = [cx.sb([128, 512], F32, "mi%d" % i) for i in range(3)]
    xi = [cx.sb([128, 512], F32, "xi%d" % i) for i in range(3)]
    ot = [cx.sb([128, 512], F32, "ot%d" % i) for i in range(3)]
    rt = cx.sb([128, 512], F32, "post_rt")
    return (mi, [Buf() for _ in range(3)], xi, [Buf() for _ in range(3)], ot, [Buf() for _ in range(3)], rt, Buf())


def build_C():
    cx = Cx()
    nc = cx.nc
    xT = cx.din("xT", [D, T + 2])
    nw1_d = cx.din("nw1", [128, KC])
    nw2_d = cx.din("nw2", [128, KC])
    wup = cx.din("wup", [D, 2 * FFD])
    cw_d = cx.din("cw", [128, FC, 3])
    cb_d = cx.din("cb", [128, FC])
    wdn = cx.din("wdn", [FFD, D])
    xo = cx.dout("xoT", [D, T])
    actS = cx.dscr("actS", [FC, 128, T], BF16)
    actSb = [Buf() for _ in range(FC)]

    nw1, nw1b = cx.load_const(nw1_d, [128, KC], name="nw1")
    nw2, nw2b = cx.load_const(nw2_d, [128, KC], name="nw2")
    cw, cwb = cx.load_const(cw_d, [128, FC, 3], name="cw")
    cbs, cbb = cx.load_const(cb_d, [128, FC], name="cbs")
    cx.arena_init(205 * 1024)

    class ASlot:
        pass

    def aslots(n, kch, name):
        r = []
        for i in range(n):
            a = ASlot()
            a.t = cx.av([128, kch, 128], BF16)
            a.b = Buf()
            a.stream = "w_%s%d" % (name, i)
            r.append(a)
        return r
    wg = aslots(2, KC, "wg")
    wvs = aslots(2, KC, "wv")

    h = cx.av([128, KC, T + 2], BF16)
    hb = [[Buf() for t in range(5)] for k in range(KC)]

    nr = make_nr(cx, cx.av)
    norm_stage(cx, nr, xT, nw1, h, lambda k, c0: hb[k][tile_of_h(c0)], halo_ranges(nr.ntw))

    wup_v = wup.rearrange("(k p) n -> p k n", p=128)
    UG = [cx.av([128, T + 2], F32) for _ in range(2)]
    ugb = [[Buf() for _ in range(5)] for _ in range(2)]
    CT = [cx.av([128, T], F32) for _ in range(2)]
    ctb = [Buf(), Buf()]
    GA = [cx.av([128, T], F32) for _ in range(2)]
    gab = [Buf(), Buf()]
    AB = [cx.av([128, T], BF16) for i in range(2)]
    abb = [Buf() for _ in range(2)]
    gws = wstream(cx, wg, [wup_v[:, :, j * 128:(j + 1) * 128] for j in range(FC)])
    vws = wstream(cx, wvs, [wup_v[:, :, FFD + j * 128:FFD + (j + 1) * 128] for j in range(FC)])
    bk = [0]

    def bank():
        b = bk[0] % 6
        bk[0] += 1
        return b

    def up_g(j):
        s = next(gws)
        p = j % 2
        for ti, (c0, w) in enumerate(TILES_H):
            b = bank()
            for k in range(KC):
                cx.mm(cx.ps[b][:, :w], s.t[:, k, :], h[:, k, c0:c0 + w], k == 0, k == KC - 1, [s.b, hb[k][ti]], [cx.psb[b]])
            cx.copy("act", UG[p][:, c0:c0 + w], cx.ps[b][:, :w], [cx.psb[b]], [ugb[p][ti]])
        cx.ts("dve", CT[p], UG[p][:, 0:T], cw[:, j, 0:1], None, ALU.mult, None, ugb[p] + [cwb], [ctb[p]])
        cx.stt(CT[p], UG[p][:, 1:T + 1], cw[:, j, 1:2], CT[p], ALU.mult, ALU.add, ugb[p] + [cwb, ctb[p]], [ctb[p]])
        cx.stt(CT[p], UG[p][:, 2:T + 2], cw[:, j, 2:3], CT[p], ALU.mult, ALU.add, ugb[p] + [cwb, ctb[p]], [ctb[p]])
        cx.act(GA[p], CT[p], AF.Gelu_apprx_tanh, [ctb[p], cbb], [gab[p]], bias=cbs[:, j:j + 1])

    def up_v(j):
        s = next(vws)
        p = j % 2
        for ti in range(1, 5):
            c0, w = TILES_H[ti]
            b = bank()
            for k in range(KC):
                cx.mm(cx.ps[b][:, :w], s.t[:, k, :], h[:, k, c0:c0 + w], k == 0, k == KC - 1, [s.b, hb[k][ti]], [cx.psb[b]])
            cx.tt("dve", AB[p][:, c0 - 2:c0 - 2 + w], GA[p][:, c0 - 2:c0 - 2 + w], cx.ps[b][:, :w], ALU.mult, [gab[p], cx.psb[b]], [abb[p]])
        cx.dma("sp", actS[j], AB[p], [abb[p]], [actSb[j]], "ab%d" % p)

    up_g(0)
    for j in range(FC):
        if j + 1 < FC:
            up_g(j + 1)
        up_v(j)

    cx.stage()
    ACT_ = cx.av([128, FC, 1024], BF16)
    atb = [Buf() for _ in range(4)]
    wdn_v = wdn.rearrange("(k p) n -> p k n", p=128)
    wd = aslots(2, FC, "wd")
    actS_v = actS.rearrange("j p t -> p j t")
    dws = wstream(cx, wd, [wdn_v[:, :, n * 128:(n + 1) * 128] for _ in range(2) for n in range(KC)])

    def load_half(hf):
        for jg in range(4):
            cx.dma("sp", ACT_[:, jg * 11:(jg + 1) * 11, :], actS_v[:, jg * 11:(jg + 1) * 11, hf * 1024:(hf + 1) * 1024],
                   actSb[jg * 11:(jg + 1) * 11], [atb[jg]], "at%d" % jg)
    down_and_post(cx, FC, dws, lambda j, ti, t: ACT_[:, j, t * 512:(t + 1) * 512], lambda j: [atb[j // 11]],
                  nw2, xT, xo, cx.av, load_half)
    cx.P.emit()
    return nc


class Slot:
    def __init__(self, cx, kch, name):
        self.t = cx.sb([128, kch, 128], BF16, name)
        self.b = Buf(name)
        self.stream = "w_" + name


def wstream(cx, slots, loads):
    def issue(i):
        s = slots[i % len(slots)]
        kch = loads[i].shape[1]
        cx.dma("pool", s.t[:, :kch, :], loads[i], [], [s.b], s.stream)
        return s
    cur = issue(0)
    for i in range(len(loads)):
        nxt = issue(i + 1) if i + 1 < len(loads) else None
        yield cur
        cur = nxt


def build_A(layer):
    cx = Cx()
    nc = cx.nc
    xT = cx.din("xT", [D, T])
    nw_d = cx.din("nw", [128, KC])
    w_in = cx.din("w_in", [D, NIN])
    lbz_d = cx.din("lbz", [128, 8, 2])
    ident_d = cx.din("ident", [128, 128])
    bm_d = cx.din("bmask", [128, 512])
    rmask_d = cx.din("rmask", [128, T])
    oloc = cx.dout("oloc", [8, 128, T])
    qBo = cx.dout("qB", [8, 128, T], BF16)
    Uc = cx.dout("Ucore", [128, 8, 128])
    Fc = cx.dout("Fcore", [128, 8])

    nw, nwb = cx.load_const(nw_d, [128, KC], name="nw")
    lbz, lbzb = cx.load_const(lbz_d, [128, 8, 2], name="lbz")
    BM, bmb = cx.load_const(bm_d, [128, 512], name="BM")
    ident = cx.sb([128, 128], BF16, "ident")
    idb = Buf()
    cx.dma("pool", ident[:], ident_d, [], [idb], "c_ident")
    rmask = cx.sb([128, T], BF16, "rmask")
    rmb = Buf()
    cx.dma("pool", rmask[:], rmask_d, [], [rmb], "c_rmask")
    onesT = cx.sb([128, T], BF16, "onesT")
    otb = Buf()
    cx.memset("pool", onesT[:], 1.0, [otb])

    sm = cx.sb([128, 8, 8], F32, "sm")
    smb = Buf()
    cx.tt("dve", sm[:, 0, :], lbz[:, :, 0], lbz[:, :, 1], ALU.max, [lbzb], [smb])
    for l in range(2):
        cx.tt("dve", sm[:, 1 + l, :], lbz[:, :, l], sm[:, 0, :], ALU.subtract, [lbzb, smb], [smb])
        cx.act(sm[:, 1 + l, :], sm[:, 1 + l, :], AF.Exp, [smb], [smb])
    cx.tt("dve", sm[:, 3, :], sm[:, 1, :], sm[:, 2, :], ALU.add, [smb], [smb])
    cx.recip(sm[:, 3, :], sm[:, 3, :], [smb], [smb])
    for l in range(2):
        cx.tt("dve", sm[:, 1 + l, :], sm[:, 1 + l, :], sm[:, 3, :], ALU.mult, [smb], [smb])
    cx.copy("dve", sm[:, 4, :], sm[:, 1, :], [smb], [smb])
    for l in range(1, layer + 1):
        cx.tt("dve", sm[:, 4, :], sm[:, 4, :], sm[:, 1 + l, :], ALU.add, [smb], [smb])
    cx.tt("dve", sm[:, 4, :], sm[:, 4, :], sm[:, 1, :], ALU.subtract, [smb], [smb])
    cx.ts("dve", sm[:, 4, :], sm[:, 4, :], 0.0, 1.0, ALU.max, ALU.min, [smb], [smb])
    cx.ts("dve", sm[:, 5, :], sm[:, 4, :], -1.0, 1.0, ALU.mult, ALU.add, [smb], [smb])
    lb = lambda hd: sm[:, 4, hd:hd + 1]
    oml = lambda hd: sm[:, 5, hd:hd + 1]

    h = cx.sb([128, KC, T], BF16, "h")
    hb = [[Buf() for t in range(4)] for k in range(KC)]
    nr = make_nr(cx, lambda sh, dt: cx.sb(sh, dt))
    norm_stage(cx, nr, xT, nw, h, lambda k, c0: hb[k][c0 // 512], plain_ranges(nr.ntw))

    Bt = [cx.sb([128, T], F32, "B%d" % i) for i in range(5)]
    Bb = [[Buf() for t in range(4)] for i in range(5)]
    KE = cx.sb([128, T], BF16, "KE"); keb = Buf()
    VT = cx.sb([128, T], BF16, "VT"); vtb = Buf()
    QE = cx.sb([128, T], BF16, "QE"); qeb = Buf()
    QB = cx.sb([128, T], BF16, "QB"); qbb = Buf()
    KT = [cx.sb([128, T], BF16, "KT%d" % i) for i in range(2)]; ktb = [Buf(), Buf()]
    VK = cx.sb([128, T], BF16, "VK"); vkb = Buf()
    PT = cx.sb([128, T], BF16, "PT"); ptb = [Buf() for _ in range(4)]
    UD = [cx.sb([128, 128], F32, "UD%d" % i) for i in range(4)]; udb = [Buf() for _ in range(4)]
    Sf = [cx.sb([128, 128], F32, "Sf%d" % i) for i in range(2)]; sfb = [Buf(), Buf()]
    Sb = [cx.sb([128, 128], BF16, "Sb%d" % i) for i in range(2)]; sbb = [Buf(), Buf()]
    UC = cx.sb([128, 8, 128], F32, "UC"); ucb = Buf()
    FCt = cx.sb([128, 8], F32, "FCt"); fcb = Buf()
    pm = cx.sb([128, 2], F32, "pm"); pmb = Buf()
    cx.memset("dve", pm[:, :], 0.0, [pmb])
    cx.memset("dve", pm[0:64, 0:1], 1.0, [pmb])
    cx.memset("dve", pm[64:128, 1:2], 1.0, [pmb])

    wv_ = w_in.rearrange("(k p) n -> p k n", p=128)
    slots = [Slot(cx, KC, "ws%d" % i) for i in range(3)]
    loads = []
    for hd in range(8):
        for base in (4096, 5120, 3072):
            loads.append(wv_[:, :, base + hd * 128:base + (hd + 1) * 128])
    ws = wstream(cx, slots, loads)
    qscale = 128.0 ** -0.5
    allB = lambda i: Bb[i]
    for hd in range(8):
        s = next(ws)
        for ti, (c0, w) in enumerate(TILES_P):
            for k in range(KC):
                cx.mm(cx.ps[ti][:], s.t[:, k, :], h[:, k, c0:c0 + w], k == 0, k == KC - 1, [s.b, hb[k][ti]], [cx.psb[ti]])
            sl = slice(c0, c0 + w)
            cx.act(Bt[0][:, sl], cx.ps[ti][:], AF.Sigmoid, [cx.psb[ti]], [Bb[0][ti]])
            cx.ts("dve", Bt[0][:, sl], Bt[0][:, sl], oml(hd), lb(hd), ALU.mult, ALU.add, [Bb[0][ti], smb], [Bb[0][ti]])
            cx.act(Bt[1][:, sl], Bt[0][:, sl], AF.Ln, [Bb[0][ti]], [Bb[1][ti]])
            cx.ts("dve", Bt[0][:, sl], Bt[0][:, sl], -1.0, 1.0, ALU.mult, ALU.add, [Bb[0][ti]], [Bb[0][ti]])
        cx.scan(Bt[2][:], rmask[:], Bt[1][:], allB(1) + [rmb], allB(2))
        cx.scan(Bt[3][:], onesT[:], Bt[1][:], allB(1) + [otb], allB(3))
        cx.act(Bt[1][:], Bt[2][:], AF.Exp, allB(2), allB(1))
        cx.act(Bt[2][:], Bt[2][:], AF.Exp, allB(2), allB(2), scale=-1.0)
        cx.act(Bt[3][:], Bt[3][:], AF.Exp, allB(3), allB(3))
        cx.tt("dve", KE[:], Bt[0][:], Bt[2][:], ALU.mult, allB(0) + allB(2), [keb])
        s = next(ws)
        for ti, (c0, w) in enumerate(TILES_P):
            for k in range(KC):
                cx.mm(cx.ps[ti][:], s.t[:, k, :], h[:, k, c0:c0 + w], k == 0, k == KC - 1, [s.b, hb[k][ti]], [cx.psb[ti]])
            cx.copy("act", VT[:, c0:c0 + w], cx.ps[ti][:], [cx.psb[ti]], [vtb])
        s = next(ws)
        for ti, (c0, w) in enumerate(TILES_P):
            for k in range(KC):
                cx.mm(cx.ps[ti][:], s.t[:, k, :], h[:, k, c0:c0 + w], k == 0, k == KC - 1, [s.b, hb[k][ti]], [cx.psb[ti]])
            cx.act(Bt[4][:, c0:c0 + w], cx.ps[ti][:], AF.Silu, [cx.psb[ti]], [Bb[4][ti]])
        cx.stt(QE[:], Bt[4][:], qscale, Bt[1][:], ALU.mult, ALU.mult, allB(4) + allB(1), [qeb])
        cx.stt(QB[:], Bt[4][:], qscale, Bt[3][:], ALU.mult, ALU.mult, allB(4) + allB(3), [qbb])
        cx.dma("sp", qBo[hd], QB[:], [qbb], [], "o_qb", final=True)
        for half in range(2):
            for j in range(8):
                blk = half * 8 + j
                cx.tr(cx.pst[:, j * 128:(j + 1) * 128], KE[:, blk * 128:(blk + 1) * 128], ident[:], [keb, idb], [cx.pstb])
            hs = slice(half * 1024, (half + 1) * 1024)
            cx.ts("dve", KT[0][:, hs], cx.pst[:], pm[:, 0:1], None, ALU.mult, None, [cx.pstb, pmb], [ktb[0]])
            cx.ts("dve", KT[1][:, hs], cx.pst[:], pm[:, 1:2], None, ALU.mult, None, [cx.pstb, pmb], [ktb[1]])
            for j in range(8):
                blk = half * 8 + j
                cx.tr(cx.pst[:, j * 128:(j + 1) * 128], VT[:, blk * 128:(blk + 1) * 128], ident[:], [vtb, idb], [cx.pstb])
            cx.copy("dve", VK[:, hs], cx.pst[:], [cx.pstb], [vkb])
        for blk in range(16):
            b = 4 + (blk // 4) % 2
            cs = slice((blk % 4) * 128, (blk % 4 + 1) * 128)
            bs = slice(blk * 128, (blk + 1) * 128)
            cx.mm(cx.ps[b][:, cs], KE[:, bs], QE[:, bs], True, True, [keb, qeb], [cx.psb[b]])
            if blk % 4 == 3:
                q4 = blk // 4
                cx.tt("dve", PT[:, q4 * 512:(q4 + 1) * 512], cx.ps[b][:], BM[:], ALU.mult, [cx.psb[b], bmb], [ptb[q4]])
        cx.memset("dve", Sf[0][:], 0.0, [sfb[0]])
        cx.memset("dve", Sb[0][:], 0.0, [sbb[0]])
        for c in range(32):
            blk = c // 2
            bs = slice(blk * 128, (blk + 1) * 128)
            cs = slice(c * 64, (c + 1) * 64)
            us = c % 4
            cur, nxt = c % 2, (c + 1) % 2
            ob = blk // 4
            oc = (blk % 4) * 128
            if c % 2 == 0:
                cx.mm(cx.ps[ob][:, oc:oc + 128], VK[:, bs], PT[:, bs], True, False, [vkb, ptb[blk // 4]], [cx.psb[ob]])
            cx.mm(cx.ps[ob][:, oc + (c % 2) * 64:oc + (c % 2) * 64 + 64], Sb[cur][:], QE[:, cs], False, c % 2 == 1,
                  [sbb[cur], qeb], [cx.psb[ob]])
            cx.mm(cx.ps[6][:, us * 128:(us + 1) * 128], KT[c % 2][:, bs], VK[:, bs], True, True, [ktb[c % 2], vkb], [cx.psb[6]])
            dcol = Bt[1][:, c * 64 + 63:c * 64 + 64]
            cx.ts("dve", UD[us][:], cx.ps[6][:, us * 128:(us + 1) * 128], dcol, None, ALU.mult, None, [cx.psb[6]] + allB(1), [udb[us]])
            cx.stt(Sf[nxt][:], Sf[cur][:], dcol, UD[us][:], ALU.mult, ALU.add, [sfb[cur], udb[us]] + allB(1), [sfb[nxt]])
            cx.copy("act", Sb[nxt][:], Sf[nxt][:], [sfb[nxt]], [sbb[nxt]])
            if c % 8 == 7:
                cx.copy("act", Bt[4][:, ob * 512:(ob + 1) * 512], cx.ps[ob][:], [cx.psb[ob]], [Bb[4][ob]])
        cx.dma("sp", oloc[hd], Bt[4][:], allB(4), [], "o_ol", final=True)
        cx.copy("dve", UC[:, hd, :], Sf[0][:], [sfb[0]], [ucb])
        cx.copy("dve", FCt[:, hd:hd + 1], Bt[3][:, T - 1:T], allB(3), [fcb])
    cx.dma("sp", Uc, UC[:], [ucb], [], "o_uc", final=True)
    cx.dma("sp", Fc, FCt[:], [fcb], [], "o_fc", final=True)
    cx.P.emit()
    return nc


def build_B():
    cx = Cx()
    nc = cx.nc
    xT = cx.din("xT", [D, T + 2])
    nw_d = cx.din("nw", [128, KC])
    nwp_d = cx.din("nwp", [128, KC])
    nwm_d = cx.din("nwm", [128, KC])
    w_in = cx.din("w_in", [D, NIN])
    cmw_d = cx.din("cmw", [128, 8, 3])
    hnw_d = cx.din("hnw", [128, 8])
    memT = cx.din("memT", [D, 256])
    wkv = cx.din("wkv", [D, 2048])
    wbr = cx.din("wbr", [3, 1024, D])
    wout = cx.din("wout", [D, D])
    oloc = cx.din("oloc", [8, 128, T])
    qBi = cx.din("qB", [8, 128, T], BF16)
    Uall = cx.din("Uall", [7, 128, 8, 128])
    Fall = cx.din("Fall", [128, 7, 8])
    ident_d = cx.din("ident", [128, 128])
    xm = cx.dout("xmT", [D, T])
    mergedS = cx.dscr("mergedS", [KC, 128, T], BF16)
    Pd = cx.dscr("Pd", [KC, 128, T], F32)
    mgb = [[Buf() for _ in range(4)] for _ in range(KC)]
    pdb = [[Buf() for _ in range(4)] for _ in range(KC)]

    nw, nwb = cx.load_const(nw_d, [128, KC], name="nw")
    nwp, nwpb = cx.load_const(nwp_d, [128, KC], name="nwp")
    nwm, nwmb = cx.load_const(nwm_d, [128, KC], name="nwm")
    cmw, cmwb = cx.load_const(cmw_d, [128, 8, 3], name="cmw")
    hnw, hnwb = cx.load_const(hnw_d, [128, 8], name="hnw")
    FA, fab = cx.load_const(Fall, [128, 7, 8], name="FA")
    ident = cx.sb([128, 128], BF16, "ident")
    idb = Buf()
    cx.dma("pool", ident[:], ident_d, [], [idb], "c_ident")

    BIG = cx.sb([128, KC * (T + 2)], BF16, "BIG")
    h = BIG.reshape([128, KC, T + 2])
    hb = [[Buf() for t in range(5)] for k in range(KC)]
    allh = [hb[k][t] for k in range(KC) for t in range(5)]
    slots = [Slot(cx, KC, "ws%d" % i) for i in range(2)]
    bslots = [Slot(cx, 8, "wb%d" % i) for i in range(2)]
    cx.arena_init(130 * 1024)
    Y0 = cx.av([128, 8, T], BF16); y0b = [[Buf() for _ in range(4)] for _ in range(8)]
    Y1 = cx.av([128, 8, T], BF16); y1b = [[Buf() for _ in range(4)] for _ in range(8)]
    base_y = cx.aoff
    MH = cx.av([128, KC, 256], BF16)
    mhb = [Buf() for _ in range(KC)]
    cx.abase = cx.aoff
    nr = make_nr(cx, cx.av)
    norm_stage(cx, nr, xT, nw, h, lambda k, c0: hb[k][tile_of_h(c0)], halo_ranges(nr.ntw))
    norm_stage(cx, nr, memT, nwm, MH, lambda k, c0: mhb[k], [(0, 256)])

    wv_ = w_in.rearrange("(k p) n -> p k n", p=128)
    wkv_v = wkv.rearrange("(k p) n -> p k n", p=128)
    wbr_v = [wbr[br].rearrange("(k p) n -> p k n", p=128) for br in range(3)]
    bk = [0]

    def bank():
        b = bk[0] % 6
        bk[0] += 1
        return b

    def branch_merge(br, Yt, Ybf, first, last):
        SGt = [cx.av([128, 512], F32) for _ in range(2)]; sgtb = [Buf(), Buf()]
        Mx = [cx.av([128, 512], F32) for _ in range(2)]; mxb = [Buf(), Buf()]
        PL = [cx.av([128, 512], F32) for _ in range(2)]; plb = [Buf(), Buf()]
        MGo = [cx.av([128, 512], BF16) for _ in range(2)]; mgob = [Buf(), Buf()]
        gl = [wv_[:, :, 8192 + br * 2048 + n * 128:8192 + br * 2048 + (n + 1) * 128] for n in range(KC)]
        bl = [wbr_v[br][:, :, n * 128:(n + 1) * 128] for n in range(KC)]
        gws = wstream(cx, slots, gl)
        bws = wstream(cx, bslots, bl)
        cnt = 0
        for n in range(KC):
            gs = next(gws)
            bs_ = next(bws)
            for ti in range(4):
                c0, w = TILES_H[ti + 1]
                tsl = slice(ti * 512, (ti + 1) * 512)
                i2 = cnt % 2
                cnt += 1
                bg, bp = bank(), bank()
                for k in range(KC):
                    cx.mm(cx.ps[bg][:], gs.t[:, k, :], h[:, k, c0:c0 + w], k == 0, k == KC - 1, [gs.b, hb[k][ti + 1]], [cx.psb[bg]])
                for k in range(8):
                    cx.mm(cx.ps[bp][:], bs_.t[:, k, :], Yt[:, k, tsl], k == 0, k == 7, [bs_.b, Ybf[k][ti]], [cx.psb[bp]])
                cx.act(SGt[i2], cx.ps[bg][:], AF.Sigmoid, [cx.psb[bg]], [sgtb[i2]])
                if not first:
                    cx.dma("sp", PL[i2], Pd[n][:, tsl], [pdb[n][ti]], [plb[i2]], "pl%d" % i2)
                cx.tt("dve", Mx[i2], SGt[i2], cx.ps[bp][:], ALU.mult, [sgtb[i2], cx.psb[bp]], [mxb[i2]])
                if first:
                    cx.dma("sp", Pd[n][:, tsl], Mx[i2], [mxb[i2]], [pdb[n][ti]], "mxo%d" % i2)
                elif not last:
                    cx.tt("dve", Mx[i2], Mx[i2], PL[i2], ALU.add, [mxb[i2], plb[i2]], [mxb[i2]])
                    cx.dma("sp", Pd[n][:, tsl], Mx[i2], [mxb[i2]], [pdb[n][ti]], "mxo%d" % i2)
                else:
                    cx.tt("dve", MGo[i2], Mx[i2], PL[i2], ALU.add, [mxb[i2], plb[i2]], [mgob[i2]])
                    cx.dma("sp", mergedS[n][:, tsl], MGo[i2], [mgob[i2]], [mgb[n][ti]], "mgo%d" % i2)

    cx.stage()
    MK = cx.av([128, 8, 256], BF16); mkb = Buf()
    MV = cx.av([128, 2, 1024], BF16); mvb = Buf()
    MQ, mqb = Y0, y0b
    YM, ymb = Y1, y1b
    loads = [wkv_v[:, :, n * 128:(n + 1) * 128] for n in range(16)] + [wv_[:, :, 7168 + n * 128:7168 + (n + 1) * 128] for n in range(8)]
    ws = wstream(cx, slots, loads)
    for n in range(8):
        s = next(ws)
        b = bank()
        for k in range(KC):
            cx.mm(cx.ps[b][:, :256], s.t[:, k, :], MH[:, k, :], k == 0, k == KC - 1, [s.b, mhb[k]], [cx.psb[b]])
        cx.copy("act", MK[:, n, :], cx.ps[b][:, :256], [cx.psb[b]], [mkb])
    for n in range(8):
        s = next(ws)
        for mb in range(2):
            b = bank()
            for k in range(KC):
                cx.mm(cx.ps[b][:, :128], MH[:, k, mb * 128:(mb + 1) * 128], s.t[:, k, :], k == 0, k == KC - 1, [s.b, mhb[k]], [cx.psb[b]])
            cx.copy("act", MV[:, mb, n * 128:(n + 1) * 128], cx.ps[b][:, :128], [cx.psb[b]], [mvb])
    for n in range(8):
        s = next(ws)
        for ti in range(1, 5):
            c0, w = TILES_H[ti]
            b = bank()
            for k in range(KC):
                cx.mm(cx.ps[b][:], s.t[:, k, :], h[:, k, c0:c0 + w], k == 0, k == KC - 1, [s.b, hb[k][ti]], [cx.psb[b]])
            cx.act(MQ[:, n, c0 - 2:c0 - 2 + w], cx.ps[b][:], AF.Copy, [cx.psb[b]], [mqb[n][ti - 1]], scale=256.0 ** -0.5)
    PTT = cx.av([128, 2, T], BF16); pttb = [Buf() for _ in range(4)]
    PE_ = [cx.av([128, 256], F32) for i in range(4)]; peb = [Buf() for _ in range(4)]
    PN = [cx.av([128, 256], BF16) for i in range(4)]; pnb = [Buf() for _ in range(4)]
    st = [cx.av([128, 4], F32) for i in range(4)]; stb = [Buf() for _ in range(4)]
    pstq = [Buf() for _ in range(4)]
    blocks = [(a, sbk) for a in range(4) for sbk in range(16)]

    def att_front(i):
        a, sbk = blocks[i]
        ti = sbk // 4
        b = bank()
        i2 = i % 4
        ss_ = slice(sbk * 128, (sbk + 1) * 128)
        for dc in range(2):
            cx.mm(cx.ps[b][:, :256], MQ[:, 2 * a + dc, ss_], MK[:, 2 * a + dc, :], dc == 0, dc == 1, [mqb[2 * a + dc][ti], mkb], [cx.psb[b]])
        cx.P.op("dve", lambda e, o=st[i2][:, 0:1], i_=cx.ps[b][:, :256]: e.tensor_reduce(out=o, in_=i_, axis=AX.X, op=ALU.max, negate=True), [cx.psb[b]], [stb[i2]])
        cx.act(PE_[i2], cx.ps[b][:, :256], AF.Exp, [cx.psb[b], stb[i2]], [peb[i2], stb[i2]], bias=st[i2][:, 0:1], accum=st[i2][:, 2:3])
        cx.recip(st[i2][:, 3:4], st[i2][:, 2:3], [stb[i2]], [stb[i2]])
        cx.ts("dve", PN[i2], PE_[i2], st[i2][:, 3:4], None, ALU.mult, None, [peb[i2], stb[i2]], [pnb[i2]])

    def att_back(i):
        a, sbk = blocks[i]
        i2 = i % 4
        q4 = i % 4
        ss_ = slice(sbk * 128, (sbk + 1) * 128)
        for mb in range(2):
            cx.tr(cx.pst[:, q4 * 256 + mb * 128:q4 * 256 + (mb + 1) * 128], PN[i2][:, mb * 128:(mb + 1) * 128], ident[:], [pnb[i2], idb], [pstq[q4]])
        cx.copy("act", PTT[:, 0, ss_], cx.pst[:, q4 * 256:q4 * 256 + 128], [pstq[q4]], [pttb0[sbk]])
        cx.copy("dve", PTT[:, 1, ss_], cx.pst[:, q4 * 256 + 128:q4 * 256 + 256], [pstq[q4]], [pttb1[sbk]])

    def att_pv(a):
        for dc in range(2):
            for ti in range(4):
                b = bank()
                tsl = slice(ti * 512, (ti + 1) * 512)
                for mb in range(2):
                    cx.mm(cx.ps[b][:], MV[:, mb, (2 * a + dc) * 128:(2 * a + dc + 1) * 128], PTT[:, mb, tsl], mb == 0, mb == 1,
                          [mvb] + pttb0[ti * 4:ti * 4 + 4] + pttb1[ti * 4:ti * 4 + 4], [cx.psb[b]])
                cx.copy("act", YM[:, 2 * a + dc, tsl], cx.ps[b][:], [cx.psb[b]], [ymb[2 * a + dc][ti]])

    pttb0 = [Buf() for _ in range(16)]
    pttb1 = [Buf() for _ in range(16)]
    LAG = 2
    for i in range(len(blocks) + LAG):
        if i < len(blocks):
            att_front(i)
        j = i - LAG
        if j >= 0:
            att_back(j)
            if j % 16 == 15:
                att_pv(j // 16)
    branch_merge(2, YM, ymb, True, False)

    cx.abase = base_y
    cx.stage()
    YA, yab = Y0, y0b
    CC = cx.av([128, T + 2], F32); ccb = [Buf() for _ in range(5)]
    UU = cx.av([128, T + 2], F32); uub = [Buf() for _ in range(5)]
    CT = cx.av([128, T], F32); ctb = Buf()
    loads = []
    for j in range(8):
        for base in (1024, 2048, 0):
            loads.append(wv_[:, :, base + j * 128:base + (j + 1) * 128])
    ws = wstream(cx, slots, loads)
    for j in range(8):
        s = next(ws)
        for ti, (c0, w) in enumerate(TILES_H):
            b = bank()
            for k in range(KC):
                cx.mm(cx.ps[b][:, :w], s.t[:, k, :], h[:, k, c0:c0 + w], k == 0, k == KC - 1, [s.b, hb[k][ti]], [cx.psb[b]])
            cx.copy("act", CC[:, c0:c0 + w], cx.ps[b][:, :w], [cx.psb[b]], [ccb[ti]])
        s = next(ws)
        for ti, (c0, w) in enumerate(TILES_H):
            b = bank()
            for k in range(KC):
                cx.mm(cx.ps[b][:, :w], s.t[:, k, :], h[:, k, c0:c0 + w], k == 0, k == KC - 1, [s.b, hb[k][ti]], [cx.psb[b]])
            cx.tt("dve", UU[:, c0:c0 + w], CC[:, c0:c0 + w], cx.ps[b][:, :w], ALU.mult, [ccb[ti], cx.psb[b]], [uub[ti]])
        cx.ts("dve", CT, UU[:, 0:T], cmw[:, j, 0:1], None, ALU.mult, None, uub + [cmwb], [ctb])
        cx.stt(CT, UU[:, 1:T + 1], cmw[:, j, 1:2], CT, ALU.mult, ALU.add, uub + [cmwb, ctb], [ctb])
        cx.stt(CT, UU[:, 2:T + 2], cmw[:, j, 2:3], CT, ALU.mult, ALU.add, uub + [cmwb, ctb], [ctb])
        s = next(ws)
        for ti in range(1, 5):
            c0, w = TILES_H[ti]
            b = bank()
            for k in range(KC):
                cx.mm(cx.ps[b][:], s.t[:, k, :], h[:, k, c0:c0 + w], k == 0, k == KC - 1, [s.b, hb[k][ti]], [cx.psb[b]])
            cx.tt("dve", YA[:, j, c0 - 2:c0 - 2 + w], CT[:, c0 - 2:c0 - 2 + w], cx.ps[b][:], ALU.mult, [ctb, cx.psb[b]], [yab[j][ti - 1]])
    branch_merge(0, YA, yab, False, False)

    cx.stage()
    YB, ybb = Y1, y1b
    UA = [cx.av([128, 7, 128], F32) for _ in range(2)]; uab = [Buf(), Buf()]
    Pst = [cx.av([128, 128], F32) for _ in range(2)]; pstb_ = [Buf(), Buf()]
    SSb = [cx.av([128, 128], BF16) for _ in range(2)]; ssbb = [Buf(), Buf()]
    OL = cx.av([128, T], F32); olb = [Buf() for _ in range(4)]
    QBt = [cx.av([128, T], BF16) for _ in range(2)]; qbtb = [Buf(), Buf()]
    SQo = [cx.av([128, 512], BF16) for i in range(2)]; sqob = [Buf(), Buf()]
    RT = [cx.av([128, 512], F32) for i in range(2)]; rtb = [Buf(), Buf()]
    SG4 = [[cx.av([128, 512], F32) for t in range(4)] for p in range(2)]
    sg4b = [[Buf() for t in range(4)] for p in range(2)]
    TMP = [cx.av([128, 512], F32) for i in range(2)]; tmpb = [Buf(), Buf()]
    loads = [wv_[:, :, 6144 + hd * 128:6144 + (hd + 1) * 128] for hd in range(8)]
    ws = wstream(cx, slots, loads)
    b4 = [0]

    def hg_proj(hd):
        s = next(ws)
        p = hd % 2
        for ti in range(4):
            c0, w = TILES_H[ti + 1]
            b2 = b4[0] % 4
            b4[0] += 1
            for k in range(KC):
                cx.mm(cx.ps[b2][:], s.t[:, k, :], h[:, k, c0:c0 + w], k == 0, k == KC - 1, [s.b, hb[k][ti + 1]], [cx.psb[b2]])
            cx.act(SG4[p][ti], cx.ps[b2][:], AF.Silu, [cx.psb[b2]], [sg4b[p][ti]])

    def prologue(hd):
        p = hd % 2
        cx.dma("sp", UA[p], Uall[:, :, hd, :].rearrange("r p v -> p r v"), [], [uab[p]], "ua%d" % p)
        cx.dma("sp", QBt[p], qBi[hd], [], [qbtb[p]], "qbt%d" % p)
        cx.memset("dve", Pst[p], 0.0, [pstb_[p]])
        for i in range(7):
            cx.stt(Pst[p], Pst[p], FA[:, i, hd:hd + 1], UA[p][:, i, :], ALU.mult, ALU.add, [pstb_[p], fab, uab[p]], [pstb_[p]])
        cx.copy("act", SSb[p], Pst[p], [pstb_[p]], [ssbb[p]])

    hg_proj(0)
    prologue(0)
    cnt = 0
    for hd in range(8):
        p = hd % 2
        for ti in range(4):
            cx.dma("sp", OL[:, ti * 512:(ti + 1) * 512], oloc[hd][:, ti * 512:(ti + 1) * 512], [], [olb[ti]], "ol%d" % ti)
        if hd + 1 < 8:
            hg_proj(hd + 1)
            prologue(hd + 1)
        for ti in range(4):
            tsl = slice(ti * 512, (ti + 1) * 512)
            i2 = cnt % 2
            cnt += 1
            b = 4 + i2
            cx.mm(cx.ps[b][:], SSb[p], QBt[p][:, tsl], True, True, [ssbb[p], qbtb[p]], [cx.psb[b]])
            cx.tt("dve", OL[:, tsl], OL[:, tsl], cx.ps[b][:], ALU.add, [olb[ti], cx.psb[b]], [olb[ti]])
            cx.act(SQo[i2], OL[:, tsl], AF.Square, [olb[ti]], [sqob[i2]])
            cx.mm(cx.ps[6][:], cx.ones[:], SQo[i2], True, True, [sqob[i2], cx.onesb], [cx.psb[6]])
            rstd_from_ss(cx, RT[i2], cx.ps[6][:], 512, 128, [cx.psb[6]], rtb[i2])
            cx.stt(TMP[i2], OL[:, tsl], hnw[:, hd:hd + 1], RT[i2], ALU.mult, ALU.mult, [olb[ti], hnwb, rtb[i2]], [tmpb[i2]])
            cx.tt("dve", YB[:, hd, tsl], TMP[i2], SG4[p][ti], ALU.mult, [tmpb[i2], sg4b[p][ti]], [ybb[hd][ti]])
    branch_merge(1, YB, ybb, False, True)

    cx.abase = 0
    cx.stage()
    mg_v = mergedS.rearrange("k p t -> p k t")
    wo_v = wout.rearrange("(k p) n -> p k n", p=128)
    MGv = BIG[:, 0:KC * T].rearrange("p (k t) -> p k t", t=T)
    mgk = [Buf() for _ in range(4)]
    for kg in range(4):
        cx.dma("sp", MGv[:, kg * 4:(kg + 1) * 4, :], mg_v[:, kg * 4:(kg + 1) * 4, :],
               [mgb[k][ti] for k in range(kg * 4, kg * 4 + 4) for ti in range(4)], [mgk[kg]] + (allh if kg == 0 else []), "at%d" % kg)
    ws = wstream(cx, slots, [wo_v[:, :, n * 128:(n + 1) * 128] for _ in range(2) for n in range(KC)])
    down_and_post(cx, KC, ws, lambda k, ti, t: MGv[:, k, ti * 512:(ti + 1) * 512], lambda k: [mgk[k // 4]],
                  nwp, xT, xm, cx.av)
    cx.P.emit()
    return nc


def _lay(v):
    return np.ascontiguousarray(np.asarray(v, np.float32).reshape(-1, 128).T)


def _consts():
    ident = np.eye(128, dtype=np.float32)
    j = np.arange(128)[:, None]
    i = np.arange(128)[None, :]
    bm = ((j // 64 == i // 64) & (j % 64 <= i % 64)).astype(np.float32)
    bmask = np.ascontiguousarray(np.tile(bm, (1, 4)))
    rmask = np.ones((128, T), np.float32)
    rmask[:, ::64] = 0.0
    return ident, bmask, rmask


def a_inputs(inp, x, l, c):
    ident, bmask, rmask = _consts()
    lbz = np.ascontiguousarray(np.asarray(inp["hg_lower_bounds"], np.float32).reshape(2, 8, 128).transpose(2, 1, 0))
    return {"xT": np.ascontiguousarray(x[c * T:(c + 1) * T].T), "nw": _lay(inp["norm_mix_pre"][l]),
            "w_in": np.asarray(inp["w_in"][l]), "lbz": lbz, "ident": ident, "bmask": bmask, "rmask": rmask}


def _with_halo_T(x, c):
    halo = x[c * T - 2:c * T] if c > 0 else np.zeros((2, D), np.float32)
    return np.ascontiguousarray(np.concatenate([halo, x[c * T:(c + 1) * T]], 0).T)


def b_inputs(inp, x, l, c, ares):
    ident, _, _ = _consts()
    Uall = np.zeros((7, 128, 8, 128), np.float32)
    Fall = np.zeros((128, 7, 8), np.float32)
    for i in range(7):
        r = c - 7 + i
        if r >= 0:
            Uall[i] = ares[r]["Ucore"]
            Fall[:, i, :] = ares[r]["Fcore"]
    return {"xT": _with_halo_T(x, c), "nw": _lay(inp["norm_mix_pre"][l]), "nwp": _lay(inp["norm_mix_post"][l]),
            "nwm": _lay(inp["norm_mem"][l]), "w_in": np.asarray(inp["w_in"][l]),
            "cmw": np.ascontiguousarray(np.asarray(inp["conv_mix_w"][l]).reshape(3, 8, 128).transpose(2, 1, 0)),
            "hnw": _lay(inp["hg_norm_w"][l]), "memT": np.ascontiguousarray(np.asarray(inp["mem"])[0].T),
            "wkv": np.asarray(inp["w_mem_kv"][l]), "wbr": np.asarray(inp["w_branch"][l]), "wout": np.asarray(inp["w_out"][l]),
            "oloc": ares[c]["oloc"], "qB": ares[c]["qB"], "Uall": Uall, "Fall": Fall, "ident": ident}


def c_inputs(inp, x, l, c):
    return {"xT": _with_halo_T(x, c), "nw1": _lay(inp["norm_ffn_pre"][l]), "nw2": _lay(inp["norm_ffn_post"][l]),
            "wup": np.asarray(inp["w_ffn_up"][l]),
            "cw": np.ascontiguousarray(np.asarray(inp["conv_ffn_w"][l]).reshape(3, FC, 128).transpose(2, 1, 0)),
            "cb": np.ascontiguousarray(np.asarray(inp["conv_ffn_b"][l]).reshape(FC, 128).T),
            "wdn": np.asarray(inp["w_ffn_down"][l])}


def _run(nc, ims):
    return run_bass_kernel_spmd(nc, ims, core_ids=list(range(NCORES))).results


def kernel(**inputs):
    inp = {k: np.asarray(v) for k, v in inputs.items()}
    x = np.ascontiguousarray(inp["x"][0].astype(np.float32))
    cores = range(NCORES)
    for l in range(2):
        ares = _run(build_A(l), [a_inputs(inp, x, l, c) for c in cores])
        bres = _run(build_B(), [b_inputs(inp, x, l, c, ares) for c in cores])
        xmid = np.ascontiguousarray(np.concatenate([np.asarray(bres[c]["xmT"]).T for c in cores], 0))
        cres = _run(build_C(), [c_inputs(inp, xmid, l, c) for c in cores])
        x = np.ascontiguousarray(np.concatenate([np.asarray(cres[c]["xoT"]).T for c in cores], 0))
    return x[None].astype(np.float32)
```

```python
import numpy as np
import ml_dtypes
import concourse.bass as bass
import concourse.mybir as mybir
from concourse.bass_utils import run_bass_kernel_spmd

F32 = mybir.dt.float32
BF16 = mybir.dt.bfloat16
AF = mybir.ActivationFunctionType
ALU = mybir.AluOpType
AX = mybir.AxisListType

D = 2048
KC = 16
T = 2048
NCORES = 8
FFD = 5632
FC = 44
NIN = 14336
EPS = 1e-6


class Buf:
    __slots__ = ("name", "lw", "rd")

    def __init__(self, name=""):
        self.name = name
        self.lw = None
        self.rd = []


class Op:
    __slots__ = ("eng", "fn", "deps", "idx", "dma", "stream", "signal", "semval", "dcount")

    def __init__(self, eng, fn, idx, dma, stream):
        self.eng = eng
        self.fn = fn
        self.idx = idx
        self.deps = []
        self.dma = dma
        self.stream = stream
        self.signal = False
        self.semval = 0
        self.dcount = 0


class Prog:
    ENGS = ("pe", "act", "dve", "pool", "sp")

    def __init__(self, nc):
        self.nc = nc
        self.ops = []
        self.streams = {}
        self.outs = []
        self.last_dma = {}

    def op(self, eng, fn, reads=(), writes=(), dma=False, stream=None):
        o = Op(eng, fn, len(self.ops), dma, stream)
        deps = set()
        for b in reads:
            if b.lw is not None:
                deps.add(b.lw)
        for b in writes:
            if b.lw is not None:
                deps.add(b.lw)
            deps.update(b.rd)
        for b in reads:
            b.rd.append(o.idx)
        for b in writes:
            b.lw = o.idx
            b.rd = []
        deps.discard(o.idx)
        o.deps = sorted(deps)
        if dma:
            self.streams[stream] = self.streams.get(stream, 0) + 1
            o.dcount = 16 * self.streams[stream]
            self.last_dma[stream] = o.idx
        self.ops.append(o)
        return o

    def dma(self, eng, out, in_, reads, writes, stream, final=False):
        o = self.op(eng, lambda e: e.dma_start(out=out, in_=in_), reads, writes, dma=True, stream=stream)
        if final:
            self.outs.append(o.idx)
        return o

    def emit(self):
        nc = self.nc
        ops = self.ops
        fin = self.op("sp", lambda e: None, [], [])
        fin.deps = sorted(set(self.outs))
        for o in ops:
            for d in o.deps:
                y = ops[d]
                if y.dma:
                    continue
                if y.eng == o.eng and o.eng in ("pe", "sp"):
                    continue
                y.signal = True
        cnt = {e: 0 for e in self.ENGS}
        for o in ops:
            if o.signal and not o.dma:
                cnt[o.eng] += 1
                o.semval = cnt[o.eng]
        esem = {e: nc.alloc_semaphore("es_" + e) for e in self.ENGS}
        ssem = {s: nc.alloc_semaphore("ds_%d" % i) for i, s in enumerate(self.streams)}
        per = {e: [o for o in ops if o.eng == e] for e in self.ENGS}

        def run(engname, e):
            seen = {}
            for o in per[engname]:
                for d in o.deps:
                    y = ops[d]
                    if y.dma:
                        sem, val = ssem[y.stream], y.dcount
                    else:
                        if y.eng == engname and engname in ("pe", "sp"):
                            continue
                        sem, val = esem[y.eng], y.semval
                    k = sem.num
                    if seen.get(k, 0) >= val:
                        continue
                    seen[k] = val
                    e.wait_ge(sem, val)
                ins = o.fn(e)
                if ins is None:
                    continue
                if o.dma:
                    ins.then_inc(ssem[o.stream], 16)
                elif o.signal:
                    ins.then_inc(esem[engname], 1)

        with nc.Block() as block:
            @block.tensor
            def _(e):
                run("pe", e)

            @block.scalar
            def _(e):
                run("act", e)

            @block.vector
            def _(e):
                run("dve", e)

            @block.gpsimd
            def _(e):
                run("pool", e)

            @block.sync
            def _(e):
                run("sp", e)


class Cx:
    def __init__(self):
        self.nc = bass.Bass("TRN2", target_bir_lowering=False)
        self.P = Prog(self.nc)
        nc = self.nc
        self.ps = [nc.alloc_psum_tensor("ps%d" % i, [128, 512], F32) for i in range(7)]
        self.psb = [Buf("ps%d" % i) for i in range(7)]
        self.pst = nc.alloc_psum_tensor("pst", [128, 1024], BF16)
        self.pstb = Buf("pst")
        self.ones = nc.alloc_sbuf_tensor("ones_bf", [128, 128], BF16)
        self.onesb = Buf("ones")
        self.epsT = nc.alloc_sbuf_tensor("epsT", [128, 1], F32)
        self.epsb = Buf("eps")
        self.memset("dve", self.ones[:], 1.0, [self.onesb])
        self.memset("dve", self.epsT[:], EPS, [self.epsb])
        self.nsb = 0

    def arena_init(self, nbytes):
        self.arena = self.sb([128, nbytes // 2], BF16, "ARENA")
        self.aoff = 0
        self.abase = 0
        self.fs = {e: self.sb([128, 1], F32, "fs_" + e) for e in ("act", "dve", "pool")}

    def stage(self):
        P = self.P
        xs = []
        t = self.fs["act"]
        xs.append(P.op("act", lambda e, t=t: e.activation(out=t[:], in_=self.epsT[:], func=AF.Copy), [self.epsb], []).idx)
        for eng in ("dve", "pool"):
            t = self.fs[eng]
            xs.append(P.op(eng, lambda e, t=t: e.memset(t[:], 0.0), [], []).idx)
        deps = sorted(set(xs + list(P.last_dma.values())))
        for eng in Prog.ENGS:
            y = P.op(eng, lambda e: None, [], [])
            y.deps = list(deps)
        self.aoff = self.abase

    def av(self, shape, dt):
        n = 1
        for d in shape[1:]:
            n *= d
        nb = n * (2 if dt == BF16 else 4)
        nb = (nb + 63) // 64 * 64
        assert self.aoff + nb <= self.arena.shape[1] * 2, ("arena overflow", self.aoff, nb)
        v = self.arena[:, self.aoff // 2:(self.aoff + nb) // 2]
        self.aoff += nb
        if dt != BF16:
            v = v.bitcast(dt)
        v = v[:, 0:n]
        if len(shape) == 3:
            v = v.rearrange("p (a b) -> p a b", b=shape[2])
        return v

    def sb(self, shape, dt, name=None):
        self.nsb += 1
        return self.nc.alloc_sbuf_tensor("s_" + (name or ("t%d" % self.nsb)), list(shape), dt)

    def din(self, name, shape, dt=F32):
        return self.nc.dram_tensor(name, list(shape), dt, kind="ExternalInput").ap()

    def dout(self, name, shape, dt=F32):
        return self.nc.dram_tensor(name, list(shape), dt, kind="ExternalOutput").ap()

    def dscr(self, name, shape, dt=F32):
        return self.nc.dram_tensor(name, list(shape), dt).ap()

    def memset(self, eng, ap, val, wr):
        self.P.op(eng, lambda e: e.memset(ap, val), [], wr)

    def mm(self, out, lhsT, rhs, start, stop, rd, wr):
        self.P.op("pe", lambda e: e.matmul(out, lhsT, rhs, start=start, stop=stop), rd, wr)

    def tr(self, out, in_, ident, rd, wr):
        self.P.op("pe", lambda e: e.transpose(out, in_, ident), rd, wr)

    def act(self, out, in_, func, rd, wr, scale=1.0, bias=None, accum=None):
        def f(e):
            kw = {}
            if bias is not None:
                kw["bias"] = bias
            if accum is not None:
                kw["accum_out"] = accum
            return e.activation(out=out, in_=in_, func=func, scale=scale, **kw)
        self.P.op("act", f, rd, wr)

    def ts(self, eng, out, in0, s1, s2, op0, op1, rd, wr):
        if s2 is None:
            self.P.op(eng, lambda e: e.tensor_scalar(out=out, in0=in0, scalar1=s1, scalar2=None, op0=op0), rd, wr)
        else:
            self.P.op(eng, lambda e: e.tensor_scalar(out=out, in0=in0, scalar1=s1, scalar2=s2, op0=op0, op1=op1), rd, wr)

    def tt(self, eng, out, in0, in1, op, rd, wr):
        self.P.op(eng, lambda e: e.tensor_tensor(out=out, in0=in0, in1=in1, op=op), rd, wr)

    def stt(self, out, in0, scalar, in1, op0, op1, rd, wr):
        self.P.op("dve", lambda e: e.scalar_tensor_tensor(out=out, in0=in0, scalar=scalar, in1=in1, op0=op0, op1=op1), rd, wr)

    def copy(self, eng, out, in_, rd, wr):
        if eng == "act":
            self.act(out, in_, AF.Copy, rd, wr)
        else:
            self.P.op(eng, lambda e: e.tensor_copy(out=out, in_=in_), rd, wr)

    def recip(self, out, in_, rd, wr):
        self.P.op("dve", lambda e: e.reciprocal(out=out, in_=in_), rd, wr)

    def scan(self, out, d0, d1, rd, wr):
        self.P.op("dve", lambda e: e.tensor_tensor_scan(out=out, data0=d0, data1=d1, initial=0.0, op0=ALU.mult, op1=ALU.add), rd, wr)

    def dma(self, eng, out, in_, rd, wr, stream, final=False):
        return self.P.dma(eng, out, in_, rd, wr, stream, final)

    def load_const(self, dram_ap, shape, dt=F32, name=None):
        t = self.sb(shape, dt, name)
        b = Buf(name or "c")
        self.dma("sp", t[:], dram_ap, [], [b], "c_%s" % (name or str(self.nsb)))
        return t, b


class NormRes:
    def __init__(self, cx, ntw=256):
        self.ntw = ntw
        self.xt = cx.sb([128, KC, ntw], F32, "n_xt")
        self.xtb = [Buf("xt%d" % g) for g in range(4)]
        self.sq = [cx.sb([128, 4, ntw], BF16, "n_sq%d" % i) for i in range(2)]
        self.sqb = [Buf("sq%d" % i) for i in range(2)]
        self.rt = cx.sb([128, ntw], F32, "n_rt")
        self.rtb = Buf("rt")
        self.cnt = 0


def rstd_from_ss(cx, out, ss, w, dim, rd, wrb):
    cx.act(out, ss, AF.Sqrt, rd + [cx.epsb], [wrb], scale=1.0 / dim, bias=cx.epsT[:, 0:1])
    cx.recip(out, out, [wrb], [wrb])


def norm_stage(cx, nr, xT, nw, nwb, h, hbuf, ranges, ssbank=6):
    xv = xT.rearrange("(k p) t -> p k t", p=128)
    ss, ssb = cx.ps[ssbank], cx.psb[ssbank]
    for ri, (c0, w) in enumerate(ranges):
        x2 = nr.rcnt % 2
        nr.rcnt += 1
        xt, xtb = nr.xt[x2], nr.xtb[x2]
        for g in range(4):
            cx.dma("sp", xt[:, g * 4:(g + 1) * 4, :w], xv[:, g * 4:(g + 1) * 4, c0:c0 + w], [], [xtb[g]], "nxt%d_%d" % (x2, g))
            s = nr.cnt % 2
            nr.cnt += 1
            cx.act(nr.sq[s][:, :, :w], xt[:, g * 4:(g + 1) * 4, :w], AF.Square, [xtb[g]], [nr.sqb[s]])
            for j in range(4):
                cx.mm(ss[:, :w], cx.ones[:], nr.sq[s][:, j, :w], g == 0 and j == 0, g == 3 and j == 3,
                      [nr.sqb[s], cx.onesb], [ssb])
        r2 = x2
        rstd_from_ss(cx, nr.rt[r2][:, :w], ss[:, :w], w, D, [ssb], nr.rtb[r2])
        for k in range(KC):
            cx.stt(h[:, k, c0:c0 + w], xt[:, k, :w], nw[:, k:k + 1], nr.rt[r2][:, :w], ALU.mult, ALU.mult,
                   [xtb[k // 4], nr.rtb[r2], nwb], [hbuf(k, c0)])


def make_nr(cx, alloc):
    class NR:
        pass
    nr = NR()
    nr.ntw = 256
    nr.xt = [alloc([128, KC, 256], F32) for _ in range(2)]
    nr.xtb = [[Buf() for g in range(4)] for _ in range(2)]
    nr.sq = [alloc([128, 4, 256], BF16) for i in range(2)]
    nr.sqb = [Buf() for i in range(2)]
    nr.rt = [alloc([128, 256], F32) for _ in range(2)]
    nr.rtb = [Buf(), Buf()]
    nr.cnt = 0
    nr.rcnt = 0
    return nr


def down_and_post(cx, kch, wsgen, rhs, rhs_rd, nw2, nw2b, xT, out, alloc, load_half=None):
    MOf = alloc([128, 2 * KC * 512], F32)
    mobuf = [[Buf() for _ in range(KC)] for _ in range(2)]
    SQ = [alloc([128, 512], BF16) for _ in range(2)]
    sqb = [Buf(), Buf()]
    xi = [alloc([128, 512], F32) for _ in range(4)]
    xib = [Buf() for _ in range(4)]
    ot = [alloc([128, 512], F32) for _ in range(2)]
    otb = [Buf() for _ in range(2)]
    rt = [alloc([128, 512], F32) for _ in range(2)]
    rtb = [Buf(), Buf()]
    cnt = 0
    pc = 0
    for hf in range(2):
        if load_half is not None:
            load_half(hf)
        for n in range(KC):
            s = next(wsgen)
            for t in range(2):
                ti = hf * 2 + t
                b = (cnt % 2) * 2 + t
                for j in range(kch):
                    cx.mm(cx.ps[b][:], s.t[:, j, :], rhs(j, ti, t), j == 0, j == kch - 1, [s.b] + rhs_rd(j), [cx.psb[b]])
                q = (cnt * 2 + t) % 2
                mo = MOf[:, (t * KC + n) * 512:(t * KC + n + 1) * 512]
                cx.copy("act", mo, cx.ps[b][:], [cx.psb[b]], [mobuf[t][n]])
                cx.act(SQ[q], cx.ps[b][:], AF.Square, [cx.psb[b]], [sqb[q]])
                cx.mm(cx.ps[4 + t][:], cx.ones[:], SQ[q], n == 0, n == KC - 1, [sqb[q], cx.onesb], [cx.psb[4 + t]])
            cnt += 1
        chunks = [(t, n) for t in range(2) for n in range(KC)]

        def load_x(i):
            t, n = chunks[i]
            ti = hf * 2 + t
            s4 = (pc + i) % 4
            cx.dma("sp", xi[s4], xT[n * 128:(n + 1) * 128, 2 + ti * 512:2 + (ti + 1) * 512], [], [xib[s4]], "xi%d" % s4)

        for i in range(3):
            load_x(i)
        for t in range(2):
            rstd_from_ss(cx, rt[t], cx.ps[4 + t][:], 512, D, [cx.psb[4 + t]], rtb[t])
        for i, (t, n) in enumerate(chunks):
            ti = hf * 2 + t
            if i + 3 < len(chunks):
                load_x(i + 3)
            s4 = (pc + i) % 4
            o2 = (pc + i) % 2
            mo = MOf[:, (t * KC + n) * 512:(t * KC + n + 1) * 512]
            cx.stt(mo, mo, nw2[:, n:n + 1], rt[t], ALU.mult, ALU.mult, [mobuf[t][n], rtb[t], nw2b], [mobuf[t][n]])
            cx.tt("dve", ot[o2], mo, xi[s4], ALU.add, [mobuf[t][n], xib[s4]], [otb[o2]])
            cx.dma("sp", out[n * 128:(n + 1) * 128, ti * 512:(ti + 1) * 512], ot[o2], [otb[o2]], [], "ot%d" % o2, final=True)
        pc += len(chunks)


def halo_ranges(ntw):
    r = [(0, 2)]
    c = 2
    while c < T + 2:
        r.append((c, ntw))
        c += ntw
    return r


def plain_ranges(ntw, total=T):
    return [(c, ntw) for c in range(0, total, ntw)]


TILES_H = [(0, 2)] + [(2 + i * 512, 512) for i in range(4)]
TILES_P = [(i * 512, 512) for i in range(4)]


def tile_of_h(c0):
    return 0 if c0 < 2 else 1 + (c0 - 2) // 512


def post_norm_residual(cx, ss, ssb, moS, mob, nw2, xT_cols, outT_cols, tagp):
    mi, mib, xi, xib, ot, otb, nr_rt, nr_rtb = tagp
    rstd_from_ss(cx, nr_rt[:, :512], ss[:, :512], 512, D, [ssb], nr_rtb)
    for n in range(KC):
        s = n % 3
        cx.dma("sp", mi[s][:, :], moS[n], [mob[n]], [mib[s]], "mi%d" % s)
        cx.dma("sp", xi[s][:, :], xT_cols(n), [], [xib[s]], "xi%d" % s)
        cx.stt(mi[s][:, :], mi[s][:, :], nw2[:, n:n + 1], nr_rt[:, :512], ALU.mult, ALU.mult, [mib[s], nr_rtb], [mib[s]])
        cx.tt("dve", ot[s][:, :], mi[s][:, :], xi[s][:, :], ALU.add, [mib[s], xib[s]], [otb[s]])
        cx.dma("sp", outT_cols(n), ot[s][:, :], [otb[s]], [], "ot%d" % s, final=True)


def alloc_post(cx):
    mi = [cx.sb([128, 512], F32, "mi%d" % i) for i in range(3)]
    xi = [cx.sb([128, 512], F32, "xi%d" % i) for i in range(3)]
    ot = [cx.sb([128, 512], F32, "ot%d" % i) for i in range(3)]
    rt = cx.sb([128, 512], F32, "post_rt")
    return (mi, [Buf() for _ in range(3)], xi, [Buf() for _ in range(3)], ot, [Buf() for _ in range(3)], rt, Buf())


def build_C():
    cx = Cx()
    nc = cx.nc
    xT = cx.din("xT", [D, T + 2])
    nw1_d = cx.din("nw1", [128, KC])
    nw2_d = cx.din("nw2", [128, KC])
    wup = cx.din("wup", [D, 2 * FFD])
    cw_d = cx.din("cw", [128, FC, 3])
    cb_d = cx.din("cb", [128, FC])
    wdn = cx.din("wdn", [FFD, D])
    xo = cx.dout("xoT", [D, T])
    actS = cx.dscr("actS", [FC, 128, T], BF16)
    actSb = [Buf() for _ in range(FC)]

    nw1, nw1b = cx.load_const(nw1_d, [128, KC], name="nw1")
    nw2, nw2b = cx.load_const(nw2_d, [128, KC], name="nw2")
    cw, cwb = cx.load_const(cw_d, [128, FC, 3], name="cw")
    cbs, cbb = cx.load_const(cb_d, [128, FC], name="cbs")
    cx.arena_init(205 * 1024)

    class ASlot:
        pass

    def aslots(n, kch, name):
        r = []
        for i in range(n):
            a = ASlot()
            a.t = cx.av([128, kch, 128], BF16)
            a.b = Buf()
            a.stream = "w_%s%d" % (name, i)
            r.append(a)
        return r
    wg = aslots(2, KC, "wg")
    wvs = aslots(2, KC, "wv")

    h = cx.av([128, KC, T + 2], BF16)
    hb = [[Buf() for t in range(5)] for k in range(KC)]

    nr = make_nr(cx, cx.av)
    norm_stage(cx, nr, xT, nw1, nw1b, h, lambda k, c0: hb[k][tile_of_h(c0)], halo_ranges(nr.ntw))

    wup_v = wup.rearrange("(k p) n -> p k n", p=128)
    UG = [cx.av([128, T + 2], F32) for _ in range(2)]
    ugb = [[Buf() for _ in range(5)] for _ in range(2)]
    CT = [cx.av([128, T], F32) for _ in range(2)]
    ctb = [Buf(), Buf()]
    GA = [cx.av([128, T], F32) for _ in range(2)]
    gab = [Buf(), Buf()]
    AB = [cx.av([128, T], BF16) for i in range(2)]
    abb = [Buf() for _ in range(2)]
    gws = wstream(cx, wg, [wup_v[:, :, j * 128:(j + 1) * 128] for j in range(FC)])
    vws = wstream(cx, wvs, [wup_v[:, :, FFD + j * 128:FFD + (j + 1) * 128] for j in range(FC)])
    bk = [0]

    def bank():
        b = bk[0] % 6
        bk[0] += 1
        return b

    def up_g(j):
        s = next(gws)
        p = j % 2
        for ti, (c0, w) in enumerate(TILES_H):
            b = bank()
            for k in range(KC):
                cx.mm(cx.ps[b][:, :w], s.t[:, k, :], h[:, k, c0:c0 + w], k == 0, k == KC - 1, [s.b, hb[k][ti]], [cx.psb[b]])
            cx.copy("act", UG[p][:, c0:c0 + w], cx.ps[b][:, :w], [cx.psb[b]], [ugb[p][ti]])
        cx.ts("dve", CT[p], UG[p][:, 0:T], cw[:, j, 0:1], None, ALU.mult, None, ugb[p] + [cwb], [ctb[p]])
        cx.stt(CT[p], UG[p][:, 1:T + 1], cw[:, j, 1:2], CT[p], ALU.mult, ALU.add, ugb[p] + [cwb, ctb[p]], [ctb[p]])
        cx.stt(CT[p], UG[p][:, 2:T + 2], cw[:, j, 2:3], CT[p], ALU.mult, ALU.add, ugb[p] + [cwb, ctb[p]], [ctb[p]])
        cx.act(GA[p], CT[p], AF.Gelu_apprx_tanh, [ctb[p], cbb], [gab[p]], bias=cbs[:, j:j + 1])

    def up_v(j):
        s = next(vws)
        p = j % 2
        for ti in range(1, 5):
            c0, w = TILES_H[ti]
            b = bank()
            for k in range(KC):
                cx.mm(cx.ps[b][:, :w], s.t[:, k, :], h[:, k, c0:c0 + w], k == 0, k == KC - 1, [s.b, hb[k][ti]], [cx.psb[b]])
            cx.tt("dve", AB[p][:, c0 - 2:c0 - 2 + w], GA[p][:, c0 - 2:c0 - 2 + w], cx.ps[b][:, :w], ALU.mult, [gab[p], cx.psb[b]], [abb[p]])
        cx.dma("sp", actS[j], AB[p], [abb[p]], [actSb[j]], "ab%d" % p)

    up_g(0)
    for j in range(FC):
        if j + 1 < FC:
            up_g(j + 1)
        up_v(j)

    cx.stage()
    ACT_ = cx.av([128, FC, 1024], BF16)
    atb = [Buf() for _ in range(4)]
    wdn_v = wdn.rearrange("(k p) n -> p k n", p=128)
    wd = aslots(2, FC, "wd")
    actS_v = actS.rearrange("j p t -> p j t")
    dws = wstream(cx, wd, [wdn_v[:, :, n * 128:(n + 1) * 128] for _ in range(2) for n in range(KC)])

    def load_half(hf):
        for jg in range(4):
            cx.dma("sp", ACT_[:, jg * 11:(jg + 1) * 11, :], actS_v[:, jg * 11:(jg + 1) * 11, hf * 1024:(hf + 1) * 1024],
                   actSb[jg * 11:(jg + 1) * 11], [atb[jg]], "at%d" % jg)
    down_and_post(cx, FC, dws, lambda j, ti, t: ACT_[:, j, t * 512:(t + 1) * 512], lambda j: [atb[j // 11]],
                  nw2, nw2b, xT, xo, cx.av, load_half)
    cx.P.emit()
    return nc


class Slot:
    def __init__(self, cx, kch, name):
        self.t = cx.sb([128, kch, 128], BF16, name)
        self.b = Buf(name)
        self.stream = "w_" + name


def wstream(cx, slots, loads):
    def issue(i):
        s = slots[i % len(slots)]
        kch = loads[i].shape[1]
        cx.dma("pool", s.t[:, :kch, :], loads[i], [], [s.b], s.stream)
        return s
    cur = issue(0)
    for i in range(len(loads)):
        nxt = issue(i + 1) if i + 1 < len(loads) else None
        yield cur
        cur = nxt


def build_A(layer):
    cx = Cx()
    nc = cx.nc
    xT = cx.din("xT", [D, T])
    nw_d = cx.din("nw", [128, KC])
    w_in = cx.din("w_in", [D, NIN])
    lbz_d = cx.din("lbz", [128, 8, 2])
    ident_d = cx.din("ident", [128, 128])
    bm_d = cx.din("bmask", [128, 512])
    rmask_d = cx.din("rmask", [128, T])
    oloc = cx.dout("oloc", [8, 128, T])
    qBo = cx.dout("qB", [8, 128, T], BF16)
    Uc = cx.dout("Ucore", [128, 8, 128])
    Fc = cx.dout("Fcore", [128, 8])

    nw, nwb = cx.load_const(nw_d, [128, KC], name="nw")
    lbz, lbzb = cx.load_const(lbz_d, [128, 8, 2], name="lbz")
    BM, bmb = cx.load_const(bm_d, [128, 512], name="BM")
    ident = cx.sb([128, 128], BF16, "ident")
    idb = Buf()
    cx.dma("pool", ident[:], ident_d, [], [idb], "c_ident")
    rmask = cx.sb([128, T], BF16, "rmask")
    rmb = Buf()
    cx.dma("pool", rmask[:], rmask_d, [], [rmb], "c_rmask")
    onesT = cx.sb([128, T], BF16, "onesT")
    otb = Buf()
    cx.memset("pool", onesT[:], 1.0, [otb])

    sm = cx.sb([128, 8, 8], F32, "sm")
    smb = Buf()
    cx.tt("dve", sm[:, 0, :], lbz[:, :, 0], lbz[:, :, 1], ALU.max, [lbzb], [smb])
    for l in range(2):
        cx.tt("dve", sm[:, 1 + l, :], lbz[:, :, l], sm[:, 0, :], ALU.subtract, [lbzb, smb], [smb])
        cx.act(sm[:, 1 + l, :], sm[:, 1 + l, :], AF.Exp, [smb], [smb])
    cx.tt("dve", sm[:, 3, :], sm[:, 1, :], sm[:, 2, :], ALU.add, [smb], [smb])
    cx.recip(sm[:, 3, :], sm[:, 3, :], [smb], [smb])
    for l in range(2):
        cx.tt("dve", sm[:, 1 + l, :], sm[:, 1 + l, :], sm[:, 3, :], ALU.mult, [smb], [smb])
    cx.copy("dve", sm[:, 4, :], sm[:, 1, :], [smb], [smb])
    for l in range(1, layer + 1):
        cx.tt("dve", sm[:, 4, :], sm[:, 4, :], sm[:, 1 + l, :], ALU.add, [smb], [smb])
    cx.tt("dve", sm[:, 4, :], sm[:, 4, :], sm[:, 1, :], ALU.subtract, [smb], [smb])
    cx.ts("dve", sm[:, 4, :], sm[:, 4, :], 0.0, 1.0, ALU.max, ALU.min, [smb], [smb])
    cx.ts("dve", sm[:, 5, :], sm[:, 4, :], -1.0, 1.0, ALU.mult, ALU.add, [smb], [smb])
    lb = lambda hd: sm[:, 4, hd:hd + 1]
    oml = lambda hd: sm[:, 5, hd:hd + 1]

    h = cx.sb([128, KC, T], BF16, "h")
    hb = [[Buf() for t in range(4)] for k in range(KC)]
    nr = make_nr(cx, lambda sh, dt: cx.sb(sh, dt))
    norm_stage(cx, nr, xT, nw, nwb, h, lambda k, c0: hb[k][c0 // 512], plain_ranges(nr.ntw))

    Bt = [cx.sb([128, T], F32, "B%d" % i) for i in range(5)]
    Bb = [[Buf() for t in range(4)] for i in range(5)]
    KE = cx.sb([128, T], BF16, "KE"); keb = Buf()
    VT = cx.sb([128, T], BF16, "VT"); vtb = Buf()
    QE = cx.sb([128, T], BF16, "QE"); qeb = Buf()
    QB = cx.sb([128, T], BF16, "QB"); qbb = Buf()
    KT = [cx.sb([128, T], BF16, "KT%d" % i) for i in range(2)]; ktb = [Buf(), Buf()]
    VK = cx.sb([128, T], BF16, "VK"); vkb = Buf()
    PT = cx.sb([128, T], BF16, "PT"); ptb = [Buf() for _ in range(4)]
    UD = [cx.sb([128, 128], F32, "UD%d" % i) for i in range(4)]; udb = [Buf() for _ in range(4)]
    Sf = [cx.sb([128, 128], F32, "Sf%d" % i) for i in range(2)]; sfb = [Buf(), Buf()]
    Sb = [cx.sb([128, 128], BF16, "Sb%d" % i) for i in range(2)]; sbb = [Buf(), Buf()]
    UC = cx.sb([128, 8, 128], F32, "UC"); ucb = Buf()
    udab = [Buf() for _ in range(32)]
    FCt = cx.sb([128, 8], F32, "FCt"); fcb = Buf()
    pm = cx.sb([128, 2], F32, "pm"); pmb = Buf()
    cx.memset("dve", pm[:, :], 0.0, [pmb])
    cx.memset("dve", pm[0:64, 0:1], 1.0, [pmb])
    cx.memset("dve", pm[64:128, 1:2], 1.0, [pmb])

    wv_ = w_in.rearrange("(k p) n -> p k n", p=128)
    slots = [Slot(cx, KC, "ws%d" % i) for i in range(3)]
    loads = []
    for hd in range(8):
        for base in (4096, 5120, 3072):
            loads.append(wv_[:, :, base + hd * 128:base + (hd + 1) * 128])
    ws = wstream(cx, slots, loads)
    qscale = 128.0 ** -0.5
    allB = lambda i: Bb[i]
    for hd in range(8):
        s = next(ws)
        for ti, (c0, w) in enumerate(TILES_P):
            for k in range(KC):
                cx.mm(cx.ps[ti][:], s.t[:, k, :], h[:, k, c0:c0 + w], k == 0, k == KC - 1, [s.b, hb[k][ti]], [cx.psb[ti]])
            sl = slice(c0, c0 + w)
            cx.act(Bt[0][:, sl], cx.ps[ti][:], AF.Sigmoid, [cx.psb[ti]], [Bb[0][ti]])
            cx.ts("dve", Bt[0][:, sl], Bt[0][:, sl], oml(hd), lb(hd), ALU.mult, ALU.add, [Bb[0][ti], smb], [Bb[0][ti]])
            cx.act(Bt[1][:, sl], Bt[0][:, sl], AF.Ln, [Bb[0][ti]], [Bb[1][ti]])
            cx.ts("dve", Bt[0][:, sl], Bt[0][:, sl], -1.0, 1.0, ALU.mult, ALU.add, [Bb[0][ti]], [Bb[0][ti]])
        cx.scan(Bt[2][:], rmask[:], Bt[1][:], allB(1) + [rmb], allB(2))
        cx.scan(Bt[3][:], onesT[:], Bt[1][:], allB(1) + [otb], allB(3))
        cx.act(Bt[1][:], Bt[2][:], AF.Exp, allB(2), allB(1))
        cx.act(Bt[2][:], Bt[2][:], AF.Exp, allB(2), allB(2), scale=-1.0)
        cx.act(Bt[3][:], Bt[3][:], AF.Exp, allB(3), allB(3))
        cx.tt("dve", KE[:], Bt[0][:], Bt[2][:], ALU.mult, allB(0) + allB(2), [keb])
        s = next(ws)
        for ti, (c0, w) in enumerate(TILES_P):
            for k in range(KC):
                cx.mm(cx.ps[ti][:], s.t[:, k, :], h[:, k, c0:c0 + w], k == 0, k == KC - 1, [s.b, hb[k][ti]], [cx.psb[ti]])
            cx.copy("act", VT[:, c0:c0 + w], cx.ps[ti][:], [cx.psb[ti]], [vtb])
        s = next(ws)
        for ti, (c0, w) in enumerate(TILES_P):
            for k in range(KC):
                cx.mm(cx.ps[ti][:], s.t[:, k, :], h[:, k, c0:c0 + w], k == 0, k == KC - 1, [s.b, hb[k][ti]], [cx.psb[ti]])
            cx.act(Bt[4][:, c0:c0 + w], cx.ps[ti][:], AF.Silu, [cx.psb[ti]], [Bb[4][ti]])
        cx.stt(QE[:], Bt[4][:], qscale, Bt[1][:], ALU.mult, ALU.mult, allB(4) + allB(1), [qeb])
        cx.stt(QB[:], Bt[4][:], qscale, Bt[3][:], ALU.mult, ALU.mult, allB(4) + allB(3), [qbb])
        cx.dma("sp", qBo[hd], QB[:], [qbb], [], "o_qb", final=True)
        for half in range(2):
            for j in range(8):
                blk = half * 8 + j
                cx.tr(cx.pst[:, j * 128:(j + 1) * 128], KE[:, blk * 128:(blk + 1) * 128], ident[:], [keb, idb], [cx.pstb])
            hs = slice(half * 1024, (half + 1) * 1024)
            cx.ts("dve", KT[0][:, hs], cx.pst[:], pm[:, 0:1], None, ALU.mult, None, [cx.pstb, pmb], [ktb[0]])
            cx.ts("dve", KT[1][:, hs], cx.pst[:], pm[:, 1:2], None, ALU.mult, None, [cx.pstb, pmb], [ktb[1]])
            for j in range(8):
                blk = half * 8 + j
                cx.tr(cx.pst[:, j * 128:(j + 1) * 128], VT[:, blk * 128:(blk + 1) * 128], ident[:], [vtb, idb], [cx.pstb])
            cx.copy("dve", VK[:, hs], cx.pst[:], [cx.pstb], [vkb])
        for blk in range(16):
            b = 4 + (blk // 4) % 2
            cs = slice((blk % 4) * 128, (blk % 4 + 1) * 128)
            bs = slice(blk * 128, (blk + 1) * 128)
            cx.mm(cx.ps[b][:, cs], KE[:, bs], QE[:, bs], True, True, [keb, qeb], [cx.psb[b]])
            if blk % 4 == 3:
                q4 = blk // 4
                cx.tt("dve", PT[:, q4 * 512:(q4 + 1) * 512], cx.ps[b][:], BM[:], ALU.mult, [cx.psb[b], bmb], [ptb[q4]])
        UDa = nr.xt[0]
        for c in range(32):
            blk = c // 2
            bs = slice(blk * 128, (blk + 1) * 128)
            ub = c % 7
            cx.mm(cx.ps[ub][:, 0:128], KT[c % 2][:, bs], VK[:, bs], True, True, [ktb[c % 2], vkb], [cx.psb[ub]])
            dcol = Bt[1][:, c * 64 + 63:c * 64 + 64]
            cx.ts("dve", UDa[:, c // 2, (c % 2) * 128:(c % 2 + 1) * 128], cx.ps[ub][:, 0:128], dcol, None, ALU.mult, None,
                  [cx.psb[ub]] + allB(1), [udab[c]])
        cx.memset("dve", Sf[0][:], 0.0, [sfb[0]])
        cx.memset("dve", Sb[0][:], 0.0, [sbb[0]])
        for c in range(32):
            blk = c // 2
            bs = slice(blk * 128, (blk + 1) * 128)
            cs = slice(c * 64, (c + 1) * 64)
            cur, nxt = c % 2, (c + 1) % 2
            ob = blk // 4
            oc = (blk % 4) * 128
            if c % 2 == 0:
                cx.mm(cx.ps[ob][:, oc:oc + 128], VK[:, bs], PT[:, bs], True, False, [vkb, ptb[blk // 4]], [cx.psb[ob]])
            cx.mm(cx.ps[ob][:, oc + (c % 2) * 64:oc + (c % 2) * 64 + 64], Sb[cur][:], QE[:, cs], False, c % 2 == 1,
                  [sbb[cur], qeb], [cx.psb[ob]])
            dcol = Bt[1][:, c * 64 + 63:c * 64 + 64]
            cx.stt(Sf[nxt][:], Sf[cur][:], dcol, UDa[:, c // 2, (c % 2) * 128:(c % 2 + 1) * 128], ALU.mult, ALU.add,
                   [sfb[cur], udab[c]] + allB(1), [sfb[nxt]])
            cx.copy("act", Sb[nxt][:], Sf[nxt][:], [sfb[nxt]], [sbb[nxt]])
            if c % 8 == 7:
                cx.copy("act", Bt[4][:, ob * 512:(ob + 1) * 512], cx.ps[ob][:], [cx.psb[ob]], [Bb[4][ob]])
        cx.dma("sp", oloc[hd], Bt[4][:], allB(4), [], "o_ol", final=True)
        cx.copy("dve", UC[:, hd, :], Sf[0][:], [sfb[0]], [ucb])
        cx.copy("dve", FCt[:, hd:hd + 1], Bt[3][:, T - 1:T], allB(3), [fcb])
    cx.dma("sp", Uc, UC[:], [ucb], [], "o_uc", final=True)
    cx.dma("sp", Fc, FCt[:], [fcb], [], "o_fc", final=True)
    cx.P.emit()
    return nc


def build_B():
    cx = Cx()
    nc = cx.nc
    xT = cx.din("xT", [D, T + 2])
    nw_d = cx.din("nw", [128, KC])
    nwp_d = cx.din("nwp", [128, KC])
    nwm_d = cx.din("nwm", [128, KC])
    w_in = cx.din("w_in", [D, NIN])
    cmw_d = cx.din("cmw", [128, 8, 3])
    hnw_d = cx.din("hnw", [128, 8])
    memT = cx.din("memT", [D, 256])
    wkv = cx.din("wkv", [D, 2048])
    wbr = cx.din("wbr", [3, 1024, D])
    wout = cx.din("wout", [D, D])
    oloc = cx.din("oloc", [8, 128, T])
    qBi = cx.din("qB", [8, 128, T], BF16)
    Uall = cx.din("Uall", [7, 128, 8, 128])
    Fall = cx.din("Fall", [128, 7, 8])
    ident_d = cx.din("ident", [128, 128])
    xm = cx.dout("xmT", [D, T])
    mergedS = cx.dscr("mergedS", [KC, 128, T], BF16)
    Pd = cx.dscr("Pd", [KC, 128, T], F32)
    mgb = [[Buf() for _ in range(4)] for _ in range(KC)]
    pdb = [[Buf() for _ in range(4)] for _ in range(KC)]

    nw, nwb = cx.load_const(nw_d, [128, KC], name="nw")
    nwp, nwpb = cx.load_const(nwp_d, [128, KC], name="nwp")
    nwm, nwmb = cx.load_const(nwm_d, [128, KC], name="nwm")
    cmw, cmwb = cx.load_const(cmw_d, [128, 8, 3], name="cmw")
    hnw, hnwb = cx.load_const(hnw_d, [128, 8], name="hnw")
    FA, fab = cx.load_const(Fall, [128, 7, 8], name="FA")
    ident = cx.sb([128, 128], BF16, "ident")
    idb = Buf()
    cx.dma("pool", ident[:], ident_d, [], [idb], "c_ident")

    BIG = cx.sb([128, KC * (T + 2)], BF16, "BIG")
    h = BIG.reshape([128, KC, T + 2])
    hb = [[Buf() for t in range(5)] for k in range(KC)]
    allh = [hb[k][t] for k in range(KC) for t in range(5)]
    slots = [Slot(cx, KC, "ws%d" % i) for i in range(2)]
    bslots = [Slot(cx, 8, "wb%d" % i) for i in range(2)]
    cx.arena_init(130 * 1024)
    Y0 = cx.av([128, 8, T], BF16); y0b = [[Buf() for _ in range(4)] for _ in range(8)]
    Y1 = cx.av([128, 8, T], BF16); y1b = [[Buf() for _ in range(4)] for _ in range(8)]
    base_y = cx.aoff
    MH = cx.av([128, KC, 256], BF16)
    mhb = [Buf() for _ in range(KC)]
    cx.abase = cx.aoff
    nr = make_nr(cx, cx.av)
    norm_stage(cx, nr, xT, nw, nwb, h, lambda k, c0: hb[k][tile_of_h(c0)], halo_ranges(nr.ntw))
    norm_stage(cx, nr, memT, nwm, nwmb, MH, lambda k, c0: mhb[k], [(0, 256)])

    wv_ = w_in.rearrange("(k p) n -> p k n", p=128)
    wkv_v = wkv.rearrange("(k p) n -> p k n", p=128)
    wbr_v = [wbr[br].rearrange("(k p) n -> p k n", p=128) for br in range(3)]
    bk = [0]

    def bank():
        b = bk[0] % 6
        bk[0] += 1
        return b

    def branch_merge(br, Yt, Ybf, first, last):
        SGt = [cx.av([128, 512], F32) for _ in range(2)]; sgtb = [Buf(), Buf()]
        Mx = [cx.av([128, 512], F32) for _ in range(2)]; mxb = [Buf(), Buf()]
        PL = [cx.av([128, 512], F32) for _ in range(2)]; plb = [Buf(), Buf()]
        MGo = [cx.av([128, 512], BF16) for _ in range(2)]; mgob = [Buf(), Buf()]
        gl = [wv_[:, :, 8192 + br * 2048 + n * 128:8192 + br * 2048 + (n + 1) * 128] for n in range(KC)]
        bl = [wbr_v[br][:, :, n * 128:(n + 1) * 128] for n in range(KC)]
        gws = wstream(cx, slots, gl)
        bws = wstream(cx, bslots, bl)
        cnt = 0
        for n in range(KC):
            gs = next(gws)
            bs_ = next(bws)
            for ti in range(4):
                c0, w = TILES_H[ti + 1]
                tsl = slice(ti * 512, (ti + 1) * 512)
                i2 = cnt % 2
                cnt += 1
                bg, bp = bank(), bank()
                for k in range(KC):
                    cx.mm(cx.ps[bg][:], gs.t[:, k, :], h[:, k, c0:c0 + w], k == 0, k == KC - 1, [gs.b, hb[k][ti + 1]], [cx.psb[bg]])
                for k in range(8):
                    cx.mm(cx.ps[bp][:], bs_.t[:, k, :], Yt[:, k, tsl], k == 0, k == 7, [bs_.b, Ybf[k][ti]], [cx.psb[bp]])
                cx.act(SGt[i2], cx.ps[bg][:], AF.Sigmoid, [cx.psb[bg]], [sgtb[i2]])
                if not first:
                    cx.dma("sp", PL[i2], Pd[n][:, tsl], [pdb[n][ti]], [plb[i2]], "pl%d" % i2)
                cx.tt("dve", Mx[i2], SGt[i2], cx.ps[bp][:], ALU.mult, [sgtb[i2], cx.psb[bp]], [mxb[i2]])
                if first:
                    cx.dma("sp", Pd[n][:, tsl], Mx[i2], [mxb[i2]], [pdb[n][ti]], "mxo%d" % i2)
                elif not last:
                    cx.tt("dve", Mx[i2], Mx[i2], PL[i2], ALU.add, [mxb[i2], plb[i2]], [mxb[i2]])
                    cx.dma("sp", Pd[n][:, tsl], Mx[i2], [mxb[i2]], [pdb[n][ti]], "mxo%d" % i2)
                else:
                    cx.tt("dve", MGo[i2], Mx[i2], PL[i2], ALU.add, [mxb[i2], plb[i2]], [mgob[i2]])
                    cx.dma("sp", mergedS[n][:, tsl], MGo[i2], [mgob[i2]], [mgb[n][ti]], "mgo%d" % i2)

    cx.stage()
    MK = cx.av([128, 8, 256], BF16); mkb = Buf()
    MV = cx.av([128, 2, 1024], BF16); mvb = Buf()
    MQ, mqb = Y0, y0b
    YM, ymb = Y1, y1b
    loads = [wkv_v[:, :, n * 128:(n + 1) * 128] for n in range(16)] + [wv_[:, :, 7168 + n * 128:7168 + (n + 1) * 128] for n in range(8)]
    ws = wstream(cx, slots, loads)
    for n in range(8):
        s = next(ws)
        b = bank()
        for k in range(KC):
            cx.mm(cx.ps[b][:, :256], s.t[:, k, :], MH[:, k, :], k == 0, k == KC - 1, [s.b, mhb[k]], [cx.psb[b]])
        cx.copy("act", MK[:, n, :], cx.ps[b][:, :256], [cx.psb[b]], [mkb])
    for n in range(8):
        s = next(ws)
        for mb in range(2):
            b = bank()
            for k in range(KC):
                cx.mm(cx.ps[b][:, :128], MH[:, k, mb * 128:(mb + 1) * 128], s.t[:, k, :], k == 0, k == KC - 1, [s.b, mhb[k]], [cx.psb[b]])
            cx.copy("act", MV[:, mb, n * 128:(n + 1) * 128], cx.ps[b][:, :128], [cx.psb[b]], [mvb])
    for n in range(8):
        s = next(ws)
        for ti in range(1, 5):
            c0, w = TILES_H[ti]
            b = bank()
            for k in range(KC):
                cx.mm(cx.ps[b][:], s.t[:, k, :], h[:, k, c0:c0 + w], k == 0, k == KC - 1, [s.b, hb[k][ti]], [cx.psb[b]])
            cx.act(MQ[:, n, c0 - 2:c0 - 2 + w], cx.ps[b][:], AF.Copy, [cx.psb[b]], [mqb[n][ti - 1]], scale=256.0 ** -0.5)
    PTT = cx.av([128, 2, T], BF16); pttb = [Buf() for _ in range(4)]
    PE_ = [cx.av([128, 256], F32) for i in range(4)]; peb = [Buf() for _ in range(4)]
    PN = [cx.av([128, 256], BF16) for i in range(4)]; pnb = [Buf() for _ in range(4)]
    st = [cx.av([128, 4], F32) for i in range(4)]; stb = [Buf() for _ in range(4)]
    pstq = [Buf() for _ in range(4)]
    blocks = [(a, sbk) for a in range(4) for sbk in range(16)]

    TB = [cx.ps[4].bitcast(BF16), cx.ps[5].bitcast(BF16), cx.ps[6].bitcast(BF16), cx.pst]
    tbb = [cx.psb[4], cx.psb[5], cx.psb[6], cx.pstb]
    ab = [0]

    def abank():
        b = ab[0] % 4
        ab[0] += 1
        return b

    def att_front(i):
        a, sbk = blocks[i]
        ti = sbk // 4
        b = abank()
        i2 = i % 4
        ss_ = slice(sbk * 128, (sbk + 1) * 128)
        for dc in range(2):
            cx.mm(cx.ps[b][:, :256], MQ[:, 2 * a + dc, ss_], MK[:, 2 * a + dc, :], dc == 0, dc == 1, [mqb[2 * a + dc][ti], mkb], [cx.psb[b]])
        cx.P.op("dve", lambda e, o=st[i2][:, 0:1], i_=cx.ps[b][:, :256]: e.tensor_reduce(out=o, in_=i_, axis=AX.X, op=ALU.max, negate=True), [cx.psb[b]], [stb[i2]])
        cx.act(PE_[i2], cx.ps[b][:, :256], AF.Exp, [cx.psb[b], stb[i2]], [peb[i2], stb[i2]], bias=st[i2][:, 0:1], accum=st[i2][:, 2:3])
        cx.recip(st[i2][:, 3:4], st[i2][:, 2:3], [stb[i2]], [stb[i2]])
        cx.ts("dve", PN[i2], PE_[i2], st[i2][:, 3:4], None, ALU.mult, None, [peb[i2], stb[i2]], [pnb[i2]])

    def att_back(i):
        a, sbk = blocks[i]
        i2 = i % 4
        q4 = i % 4
        ss_ = slice(sbk * 128, (sbk + 1) * 128)
        for mb in range(2):
            cx.tr(TB[q4][:, mb * 128:(mb + 1) * 128], PN[i2][:, mb * 128:(mb + 1) * 128], ident[:], [pnb[i2], idb], [tbb[q4]])
        eng = "act" if i % 2 == 0 else "dve"
        cx.copy(eng, PTT[:, :, ss_], TB[q4][:, 0:256].rearrange("p (m s) -> p m s", s=128), [tbb[q4]], [pttb0[sbk], pttb1[sbk]])

    def att_pv(a):
        for dc in range(2):
            for ti in range(4):
                b = abank()
                tsl = slice(ti * 512, (ti + 1) * 512)
                for mb in range(2):
                    cx.mm(cx.ps[b][:], MV[:, mb, (2 * a + dc) * 128:(2 * a + dc + 1) * 128], PTT[:, mb, tsl], mb == 0, mb == 1,
                          [mvb] + pttb0[ti * 4:ti * 4 + 4] + pttb1[ti * 4:ti * 4 + 4], [cx.psb[b]])
                cx.copy("act", YM[:, 2 * a + dc, tsl], cx.ps[b][:], [cx.psb[b]], [ymb[2 * a + dc][ti]])

    pttb0 = [Buf() for _ in range(16)]
    pttb1 = [Buf() for _ in range(16)]
    LAG = 2
    for i in range(len(blocks) + LAG):
        if i < len(blocks):
            att_front(i)
        j = i - LAG
        if j >= 0:
            att_back(j)
            if j % 16 == 15:
                att_pv(j // 16)
    branch_merge(2, YM, ymb, True, False)

    cx.abase = base_y
    cx.stage()
    YA, yab = Y0, y0b
    CC = cx.av([128, T + 2], F32); ccb = [Buf() for _ in range(5)]
    UU = cx.av([128, T + 2], F32); uub = [Buf() for _ in range(5)]
    CT = cx.av([128, T], F32); ctb = Buf()
    loads = []
    for j in range(8):
        for base in (1024, 2048, 0):
            loads.append(wv_[:, :, base + j * 128:base + (j + 1) * 128])
    ws = wstream(cx, slots, loads)
    for j in range(8):
        s = next(ws)
        for ti, (c0, w) in enumerate(TILES_H):
            b = bank()
            for k in range(KC):
                cx.mm(cx.ps[b][:, :w], s.t[:, k, :], h[:, k, c0:c0 + w], k == 0, k == KC - 1, [s.b, hb[k][ti]], [cx.psb[b]])
            cx.copy("act", CC[:, c0:c0 + w], cx.ps[b][:, :w], [cx.psb[b]], [ccb[ti]])
        s = next(ws)
        for ti, (c0, w) in enumerate(TILES_H):
            b = bank()
            for k in range(KC):
                cx.mm(cx.ps[b][:, :w], s.t[:, k, :], h[:, k, c0:c0 + w], k == 0, k == KC - 1, [s.b, hb[k][ti]], [cx.psb[b]])
            cx.tt("dve", UU[:, c0:c0 + w], CC[:, c0:c0 + w], cx.ps[b][:, :w], ALU.mult, [ccb[ti], cx.psb[b]], [uub[ti]])
        cx.ts("dve", CT, UU[:, 0:T], cmw[:, j, 0:1], None, ALU.mult, None, uub + [cmwb], [ctb])
        cx.stt(CT, UU[:, 1:T + 1], cmw[:, j, 1:2], CT, ALU.mult, ALU.add, uub + [cmwb, ctb], [ctb])
        cx.stt(CT, UU[:, 2:T + 2], cmw[:, j, 2:3], CT, ALU.mult, ALU.add, uub + [cmwb, ctb], [ctb])
        s = next(ws)
        for ti in range(1, 5):
            c0, w = TILES_H[ti]
            b = bank()
            for k in range(KC):
                cx.mm(cx.ps[b][:], s.t[:, k, :], h[:, k, c0:c0 + w], k == 0, k == KC - 1, [s.b, hb[k][ti]], [cx.psb[b]])
            cx.tt("dve", YA[:, j, c0 - 2:c0 - 2 + w], CT[:, c0 - 2:c0 - 2 + w], cx.ps[b][:], ALU.mult, [ctb, cx.psb[b]], [yab[j][ti - 1]])
    branch_merge(0, YA, yab, False, False)

    cx.stage()
    YB, ybb = Y1, y1b
    UA = [cx.av([128, 7, 128], F32) for _ in range(2)]; uab = [Buf(), Buf()]
    Pst = [cx.av([128, 128], F32) for _ in range(2)]; pstb_ = [Buf(), Buf()]
    SSb = [cx.av([128, 128], BF16) for _ in range(2)]; ssbb = [Buf(), Buf()]
    OL = cx.av([128, T], F32); olb = [Buf() for _ in range(4)]
    QBt = [cx.av([128, T], BF16) for _ in range(2)]; qbtb = [Buf(), Buf()]
    SQo = [cx.av([128, 512], BF16) for i in range(2)]; sqob = [Buf(), Buf()]
    RT = [cx.av([128, 512], F32) for i in range(2)]; rtb = [Buf(), Buf()]
    SG4 = [[cx.av([128, 512], F32) for t in range(4)] for p in range(2)]
    sg4b = [[Buf() for t in range(4)] for p in range(2)]
    TMP = [cx.av([128, 512], F32) for i in range(2)]; tmpb = [Buf(), Buf()]
    loads = [wv_[:, :, 6144 + hd * 128:6144 + (hd + 1) * 128] for hd in range(8)]
    ws = wstream(cx, slots, loads)
    b4 = [0]

    def hg_proj(hd):
        s = next(ws)
        p = hd % 2
        for ti in range(4):
            c0, w = TILES_H[ti + 1]
            b2 = b4[0] % 4
            b4[0] += 1
            for k in range(KC):
                cx.mm(cx.ps[b2][:], s.t[:, k, :], h[:, k, c0:c0 + w], k == 0, k == KC - 1, [s.b, hb[k][ti + 1]], [cx.psb[b2]])
            cx.act(SG4[p][ti], cx.ps[b2][:], AF.Silu, [cx.psb[b2]], [sg4b[p][ti]])

    def prologue(hd):
        p = hd % 2
        cx.dma("sp", UA[p], Uall[:, :, hd, :].rearrange("r p v -> p r v"), [], [uab[p]], "ua%d" % p)
        cx.dma("sp", QBt[p], qBi[hd], [], [qbtb[p]], "qbt%d" % p)
        cx.memset("dve", Pst[p], 0.0, [pstb_[p]])
        for i in range(7):
            cx.stt(Pst[p], Pst[p], FA[:, i, hd:hd + 1], UA[p][:, i, :], ALU.mult, ALU.add, [pstb_[p], fab, uab[p]], [pstb_[p]])
        cx.copy("act", SSb[p], Pst[p], [pstb_[p]], [ssbb[p]])

    hg_proj(0)
    prologue(0)
    cnt = 0
    for hd in range(8):
        p = hd % 2
        for ti in range(4):
            cx.dma("sp", OL[:, ti * 512:(ti + 1) * 512], oloc[hd][:, ti * 512:(ti + 1) * 512], [], [olb[ti]], "ol%d" % ti)
        if hd + 1 < 8:
            hg_proj(hd + 1)
            prologue(hd + 1)
        for ti in range(4):
            tsl = slice(ti * 512, (ti + 1) * 512)
            i2 = cnt % 2
            cnt += 1
            b = 4 + i2
            cx.mm(cx.ps[b][:], SSb[p], QBt[p][:, tsl], True, True, [ssbb[p], qbtb[p]], [cx.psb[b]])
            cx.tt("dve", OL[:, tsl], OL[:, tsl], cx.ps[b][:], ALU.add, [olb[ti], cx.psb[b]], [olb[ti]])
            cx.act(SQo[i2], OL[:, tsl], AF.Square, [olb[ti]], [sqob[i2]])
            cx.mm(cx.ps[6][:], cx.ones[:], SQo[i2], True, True, [sqob[i2], cx.onesb], [cx.psb[6]])
            rstd_from_ss(cx, RT[i2], cx.ps[6][:], 512, 128, [cx.psb[6]], rtb[i2])
            cx.stt(TMP[i2], OL[:, tsl], hnw[:, hd:hd + 1], RT[i2], ALU.mult, ALU.mult, [olb[ti], hnwb, rtb[i2]], [tmpb[i2]])
            cx.tt("dve", YB[:, hd, tsl], TMP[i2], SG4[p][ti], ALU.mult, [tmpb[i2], sg4b[p][ti]], [ybb[hd][ti]])
    branch_merge(1, YB, ybb, False, True)

    cx.abase = 0
    cx.stage()
    mg_v = mergedS.rearrange("k p t -> p k t")
    wo_v = wout.rearrange("(k p) n -> p k n", p=128)
    MGv = BIG[:, 0:KC * T].rearrange("p (k t) -> p k t", t=T)
    mgk = [Buf() for _ in range(4)]
    for kg in range(4):
        cx.dma("sp", MGv[:, kg * 4:(kg + 1) * 4, :], mg_v[:, kg * 4:(kg + 1) * 4, :],
               [mgb[k][ti] for k in range(kg * 4, kg * 4 + 4) for ti in range(4)], [mgk[kg]] + (allh if kg == 0 else []), "at%d" % kg)
    ws = wstream(cx, slots, [wo_v[:, :, n * 128:(n + 1) * 128] for _ in range(2) for n in range(KC)])
    down_and_post(cx, KC, ws, lambda k, ti, t: MGv[:, k, ti * 512:(ti + 1) * 512], lambda k: [mgk[k // 4]],
                  nwp, nwpb, xT, xm, cx.av)
    cx.P.emit()
    return nc


def _lay(v):
    return np.ascontiguousarray(np.asarray(v, np.float32).reshape(-1, 128).T)


def _consts():
    ident = np.eye(128, dtype=np.float32)
    j = np.arange(128)[:, None]
    i = np.arange(128)[None, :]
    bm = ((j // 64 == i // 64) & (j % 64 <= i % 64)).astype(np.float32)
    bmask = np.ascontiguousarray(np.tile(bm, (1, 4)))
    rmask = np.ones((128, T), np.float32)
    rmask[:, ::64] = 0.0
    return ident, bmask, rmask


def a_inputs(inp, x, l, c):
    ident, bmask, rmask = _consts()
    lbz = np.ascontiguousarray(np.asarray(inp["hg_lower_bounds"], np.float32).reshape(2, 8, 128).transpose(2, 1, 0))
    return {"xT": np.ascontiguousarray(x[c * T:(c + 1) * T].T), "nw": _lay(inp["norm_mix_pre"][l]),
            "w_in": np.asarray(inp["w_in"][l]), "lbz": lbz, "ident": ident, "bmask": bmask, "rmask": rmask}


def _with_halo_T(x, c):
    halo = x[c * T - 2:c * T] if c > 0 else np.zeros((2, D), np.float32)
    return np.ascontiguousarray(np.concatenate([halo, x[c * T:(c + 1) * T]], 0).T)


def b_inputs(inp, x, l, c, ares):
    ident, _, _ = _consts()
    Uall = np.zeros((7, 128, 8, 128), np.float32)
    Fall = np.zeros((128, 7, 8), np.float32)
    for i in range(7):
        r = c - 7 + i
        if r >= 0:
            Uall[i] = ares[r]["Ucore"]
            Fall[:, i, :] = ares[r]["Fcore"]
    return {"xT": _with_halo_T(x, c), "nw": _lay(inp["norm_mix_pre"][l]), "nwp": _lay(inp["norm_mix_post"][l]),
            "nwm": _lay(inp["norm_mem"][l]), "w_in": np.asarray(inp["w_in"][l]),
            "cmw": np.ascontiguousarray(np.asarray(inp["conv_mix_w"][l]).reshape(3, 8, 128).transpose(2, 1, 0)),
            "hnw": _lay(inp["hg_norm_w"][l]), "memT": np.ascontiguousarray(np.asarray(inp["mem"])[0].T),
            "wkv": np.asarray(inp["w_mem_kv"][l]), "wbr": np.asarray(inp["w_branch"][l]), "wout": np.asarray(inp["w_out"][l]),
            "oloc": ares[c]["oloc"], "qB": ares[c]["qB"], "Uall": Uall, "Fall": Fall, "ident": ident}


def c_inputs(inp, x, l, c):
    return {"xT": _with_halo_T(x, c), "nw1": _lay(inp["norm_ffn_pre"][l]), "nw2": _lay(inp["norm_ffn_post"][l]),
            "wup": np.asarray(inp["w_ffn_up"][l]),
            "cw": np.ascontiguousarray(np.asarray(inp["conv_ffn_w"][l]).reshape(3, FC, 128).transpose(2, 1, 0)),
            "cb": np.ascontiguousarray(np.asarray(inp["conv_ffn_b"][l]).reshape(FC, 128).T),
            "wdn": np.asarray(inp["w_ffn_down"][l])}


def _run(nc, ims):
    return run_bass_kernel_spmd(nc, ims, core_ids=list(range(NCORES))).results


def kernel(**inputs):
    inp = {k: np.asarray(v) for k, v in inputs.items()}
    x = np.ascontiguousarray(inp["x"][0].astype(np.float32))
    cores = range(NCORES)
    for l in range(2):
        ares = _run(build_A(l), [a_inputs(inp, x, l, c) for c in cores])
        bres = _run(build_B(), [b_inputs(inp, x, l, c, ares) for c in cores])
        xmid = np.ascontiguousarray(np.concatenate([np.asarray(bres[c]["xmT"]).T for c in cores], 0))
        cres = _run(build_C(), [c_inputs(inp, xmid, l, c) for c in cores])
        x = np.ascontiguousarray(np.concatenate([np.asarray(cres[c]["xoT"]).T for c in cores], 0))
    return x[None].astype(np.float32)
```

```python
import numpy as np
import ml_dtypes
import concourse.bass as bass
import concourse.mybir as mybir
from concourse.bass_utils import run_bass_kernel_spmd

F32 = mybir.dt.float32
BF16 = mybir.dt.bfloat16
AF = mybir.ActivationFunctionType
ALU = mybir.AluOpType
AX = mybir.AxisListType

D = 2048
KC = 16
T = 2048
NCORES = 8
FFD = 5632
FC = 44
NIN = 14336
EPS = 1e-6


class Buf:
    __slots__ = ("name", "lw", "rd")

    def __init__(self, name=""):
        self.name = name
        self.lw = None
        self.rd = []


class Op:
    __slots__ = ("eng", "fn", "deps", "idx", "dma", "stream", "signal", "semval", "dcount")

    def __init__(self, eng, fn, idx, dma, stream):
        self.eng = eng
        self.fn = fn
        self.idx = idx
        self.deps = []
        self.dma = dma
        self.stream = stream
        self.signal = False
        self.semval = 0
        self.dcount = 0


class Prog:
    ENGS = ("pe", "act", "dve", "pool", "sp")

    def __init__(self, nc):
        self.nc = nc
        self.ops = []
        self.streams = {}
        self.outs = []
        self.last_dma = {}

    def op(self, eng, fn, reads=(), writes=(), dma=False, stream=None):
        o = Op(eng, fn, len(self.ops), dma, stream)
        deps = set()
        for b in reads:
            if b.lw is not None:
                deps.add(b.lw)
        for b in writes:
            if b.lw is not None:
                deps.add(b.lw)
            deps.update(b.rd)
        for b in reads:
            b.rd.append(o.idx)
        for b in writes:
            b.lw = o.idx
            b.rd = []
        deps.discard(o.idx)
        o.deps = sorted(deps)
        if dma:
            self.streams[stream] = self.streams.get(stream, 0) + 1
            o.dcount = 16 * self.streams[stream]
            self.last_dma[stream] = o.idx
        self.ops.append(o)
        return o

    def dma(self, eng, out, in_, reads, writes, stream, final=False):
        o = self.op(eng, lambda e: e.dma_start(out=out, in_=in_), reads, writes, dma=True, stream=stream)
        if final:
            self.outs.append(o.idx)
        return o

    def emit(self):
        nc = self.nc
        ops = self.ops
        fin = self.op("sp", lambda e: None, [], [])
        fin.deps = sorted(set(self.outs))
        for o in ops:
            for d in o.deps:
                y = ops[d]
                if y.dma:
                    continue
                if y.eng == o.eng and o.eng in ("pe", "sp"):
                    continue
                y.signal = True
        cnt = {e: 0 for e in self.ENGS}
        for o in ops:
            if o.signal and not o.dma:
                cnt[o.eng] += 1
                o.semval = cnt[o.eng]
        esem = {e: nc.alloc_semaphore("es_" + e) for e in self.ENGS}
        ssem = {s: nc.alloc_semaphore("ds_%d" % i) for i, s in enumerate(self.streams)}
        per = {e: [o for o in ops if o.eng == e] for e in self.ENGS}

        def run(engname, e):
            seen = {}
            for o in per[engname]:
                for d in o.deps:
                    y = ops[d]
                    if y.dma:
                        sem, val = ssem[y.stream], y.dcount
                    else:
                        if y.eng == engname and engname in ("pe", "sp"):
                            continue
                        sem, val = esem[y.eng], y.semval
                    k = sem.num
                    if seen.get(k, 0) >= val:
                        continue
                    seen[k] = val
                    e.wait_ge(sem, val)
                ins = o.fn(e)
                if ins is None:
                    continue
                if o.dma:
                    ins.then_inc(ssem[o.stream], 16)
                elif o.signal:
                    ins.then_inc(esem[engname], 1)

        with nc.Block() as block:
            @block.tensor
            def _(e):
                run("pe", e)

            @block.scalar
            def _(e):
                run("act", e)

            @block.vector
            def _(e):
                run("dve", e)

            @block.gpsimd
            def _(e):
                run("pool", e)

            @block.sync
            def _(e):
                run("sp", e)


class Cx:
    def __init__(self):
        self.nc = bass.Bass("TRN2", target_bir_lowering=False)
        self.P = Prog(self.nc)
        nc = self.nc
        self.ps = [nc.alloc_psum_tensor("ps%d" % i, [128, 512], F32) for i in range(7)]
        self.psb = [Buf("ps%d" % i) for i in range(7)]
        self.pst = nc.alloc_psum_tensor("pst", [128, 1024], BF16)
        self.pstb = Buf("pst")
        self.ones = nc.alloc_sbuf_tensor("ones_bf", [128, 128], BF16)
        self.onesb = Buf("ones")
        self.epsT = nc.alloc_sbuf_tensor("epsT", [128, 1], F32)
        self.epsb = Buf("eps")
        self.memset("dve", self.ones[:], 1.0, [self.onesb])
        self.memset("dve", self.epsT[:], EPS, [self.epsb])
        self.nsb = 0

    def arena_init(self, nbytes):
        self.arena = self.sb([128, nbytes // 2], BF16, "ARENA")
        self.aoff = 0
        self.abase = 0
        self.fs = {e: self.sb([128, 1], F32, "fs_" + e) for e in ("act", "dve", "pool")}

    def stage(self):
        P = self.P
        xs = []
        t = self.fs["act"]
        xs.append(P.op("act", lambda e, t=t: e.activation(out=t[:], in_=self.epsT[:], func=AF.Copy), [self.epsb], []).idx)
        for eng in ("dve", "pool"):
            t = self.fs[eng]
            xs.append(P.op(eng, lambda e, t=t: e.memset(t[:], 0.0), [], []).idx)
        deps = sorted(set(xs + list(P.last_dma.values())))
        for eng in Prog.ENGS:
            y = P.op(eng, lambda e: None, [], [])
            y.deps = list(deps)
        self.aoff = self.abase

    def av(self, shape, dt):
        n = 1
        for d in shape[1:]:
            n *= d
        nb = n * (2 if dt == BF16 else 4)
        nb = (nb + 63) // 64 * 64
        assert self.aoff + nb <= self.arena.shape[1] * 2, ("arena overflow", self.aoff, nb)
        v = self.arena[:, self.aoff // 2:(self.aoff + nb) // 2]
        self.aoff += nb
        if dt != BF16:
            v = v.bitcast(dt)
        v = v[:, 0:n]
        if len(shape) == 3:
            v = v.rearrange("p (a b) -> p a b", b=shape[2])
        return v

    def sb(self, shape, dt, name=None):
        self.nsb += 1
        return self.nc.alloc_sbuf_tensor("s_" + (name or ("t%d" % self.nsb)), list(shape), dt)

    def din(self, name, shape, dt=F32):
        return self.nc.dram_tensor(name, list(shape), dt, kind="ExternalInput").ap()

    def dout(self, name, shape, dt=F32):
        return self.nc.dram_tensor(name, list(shape), dt, kind="ExternalOutput").ap()

    def dscr(self, name, shape, dt=F32):
        return self.nc.dram_tensor(name, list(shape), dt).ap()

    def memset(self, eng, ap, val, wr):
        self.P.op(eng, lambda e: e.memset(ap, val), [], wr)

    def mm(self, out, lhsT, rhs, start, stop, rd, wr):
        self.P.op("pe", lambda e: e.matmul(out, lhsT, rhs, start=start, stop=stop), rd, wr)

    def tr(self, out, in_, ident, rd, wr):
        self.P.op("pe", lambda e: e.transpose(out, in_, ident), rd, wr)

    def act(self, out, in_, func, rd, wr, scale=1.0, bias=None, accum=None):
        def f(e):
            kw = {}
            if bias is not None:
                kw["bias"] = bias
            if accum is not None:
                kw["accum_out"] = accum
            return e.activation(out=out, in_=in_, func=func, scale=scale, **kw)
        self.P.op("act", f, rd, wr)

    def ts(self, eng, out, in0, s1, s2, op0, op1, rd, wr):
        if s2 is None:
            self.P.op(eng, lambda e: e.tensor_scalar(out=out, in0=in0, scalar1=s1, scalar2=None, op0=op0), rd, wr)
        else:
            self.P.op(eng, lambda e: e.tensor_scalar(out=out, in0=in0, scalar1=s1, scalar2=s2, op0=op0, op1=op1), rd, wr)

    def tt(self, eng, out, in0, in1, op, rd, wr):
        self.P.op(eng, lambda e: e.tensor_tensor(out=out, in0=in0, in1=in1, op=op), rd, wr)

    def stt(self, out, in0, scalar, in1, op0, op1, rd, wr):
        self.P.op("dve", lambda e: e.scalar_tensor_tensor(out=out, in0=in0, scalar=scalar, in1=in1, op0=op0, op1=op1), rd, wr)

    def copy(self, eng, out, in_, rd, wr):
        if eng == "act":
            self.act(out, in_, AF.Copy, rd, wr)
        else:
            self.P.op(eng, lambda e: e.tensor_copy(out=out, in_=in_), rd, wr)

    def recip(self, out, in_, rd, wr):
        self.P.op("dve", lambda e: e.reciprocal(out=out, in_=in_), rd, wr)

    def scan(self, out, d0, d1, rd, wr):
        self.P.op("dve", lambda e: e.tensor_tensor_scan(out=out, data0=d0, data1=d1, initial=0.0, op0=ALU.mult, op1=ALU.add), rd, wr)

    def dma(self, eng, out, in_, rd, wr, stream, final=False):
        return self.P.dma(eng, out, in_, rd, wr, stream, final)

    def load_const(self, dram_ap, shape, dt=F32, name=None):
        t = self.sb(shape, dt, name)
        b = Buf(name or "c")
        self.dma("sp", t[:], dram_ap, [], [b], "c_%s" % (name or str(self.nsb)))
        return t, b


class NormRes:
    def __init__(self, cx, ntw=256):
        self.ntw = ntw
        self.xt = cx.sb([128, KC, ntw], F32, "n_xt")
        self.xtb = [Buf("xt%d" % g) for g in range(4)]
        self.sq = [cx.sb([128, 4, ntw], BF16, "n_sq%d" % i) for i in range(2)]
        self.sqb = [Buf("sq%d" % i) for i in range(2)]
        self.rt = cx.sb([128, ntw], F32, "n_rt")
        self.rtb = Buf("rt")
        self.cnt = 0


def rstd_from_ss(cx, out, ss, w, dim, rd, wrb):
    cx.act(out, ss, AF.Sqrt, rd + [cx.epsb], [wrb], scale=1.0 / dim, bias=cx.epsT[:, 0:1])
    cx.recip(out, out, [wrb], [wrb])


def norm_stage(cx, nr, xT, nw, nwb, h, hbuf, ranges, ssbank=6):
    xv = xT.rearrange("(k p) t -> p k t", p=128)
    ss, ssb = cx.ps[ssbank], cx.psb[ssbank]
    for ri, (c0, w) in enumerate(ranges):
        x2 = nr.rcnt % 2
        nr.rcnt += 1
        xt, xtb = nr.xt[x2], nr.xtb[x2]
        for g in range(4):
            cx.dma("sp", xt[:, g * 4:(g + 1) * 4, :w], xv[:, g * 4:(g + 1) * 4, c0:c0 + w], [], [xtb[g]], "nxt%d_%d" % (x2, g))
            s = nr.cnt % 2
            nr.cnt += 1
            cx.act(nr.sq[s][:, :, :w], xt[:, g * 4:(g + 1) * 4, :w], AF.Square, [xtb[g]], [nr.sqb[s]])
            for j in range(4):
                cx.mm(ss[:, :w], cx.ones[:], nr.sq[s][:, j, :w], g == 0 and j == 0, g == 3 and j == 3,
                      [nr.sqb[s], cx.onesb], [ssb])
        r2 = x2
        rstd_from_ss(cx, nr.rt[r2][:, :w], ss[:, :w], w, D, [ssb], nr.rtb[r2])
        for k in range(KC):
            cx.stt(h[:, k, c0:c0 + w], xt[:, k, :w], nw[:, k:k + 1], nr.rt[r2][:, :w], ALU.mult, ALU.mult,
                   [xtb[k // 4], nr.rtb[r2], nwb], [hbuf(k, c0)])


def make_nr(cx, alloc):
    class NR:
        pass
    nr = NR()
    nr.ntw = 256
    nr.xt = [alloc([128, KC, 256], F32) for _ in range(2)]
    nr.xtb = [[Buf() for g in range(4)] for _ in range(2)]
    nr.sq = [alloc([128, 4, 256], BF16) for i in range(2)]
    nr.sqb = [Buf() for i in range(2)]
    nr.rt = [alloc([128, 256], F32) for _ in range(2)]
    nr.rtb = [Buf(), Buf()]
    nr.cnt = 0
    nr.rcnt = 0
    return nr


def down_and_post(cx, kch, wsgen, rhs, rhs_rd, nw2, nw2b, xT, out, alloc, load_half=None, n_ot=3):
    MOf = alloc([128, 2 * KC * 512], F32)
    mobuf = [[Buf() for _ in range(KC)] for _ in range(2)]
    SQ = [alloc([128, 512], BF16) for _ in range(2)]
    sqb = [Buf(), Buf()]
    xi = [alloc([128, 512], F32) for _ in range(4)]
    xib = [Buf() for _ in range(4)]
    ot = [alloc([128, 512], F32) for _ in range(n_ot)]
    otb = [Buf() for _ in range(n_ot)]
    rt = [alloc([128, 512], F32) for _ in range(2)]
    rtb = [Buf(), Buf()]
    cnt = 0
    pc = 0
    for hf in range(2):
        if load_half is not None:
            load_half(hf)
        for n in range(KC):
            s = next(wsgen)
            for t in range(2):
                ti = hf * 2 + t
                b = (cnt % 2) * 2 + t
                for j in range(kch):
                    cx.mm(cx.ps[b][:], s.t[:, j, :], rhs(j, ti, t), j == 0, j == kch - 1, [s.b] + rhs_rd(j), [cx.psb[b]])
                q = (cnt * 2 + t) % 2
                mo = MOf[:, (t * KC + n) * 512:(t * KC + n + 1) * 512]
                cx.copy("act", mo, cx.ps[b][:], [cx.psb[b]], [mobuf[t][n]])
                cx.act(SQ[q], cx.ps[b][:], AF.Square, [cx.psb[b]], [sqb[q]])
                cx.mm(cx.ps[4 + t][:], cx.ones[:], SQ[q], n == 0, n == KC - 1, [sqb[q], cx.onesb], [cx.psb[4 + t]])
            cnt += 1
        chunks = [(t, n) for t in range(2) for n in range(KC)]

        def load_x(i):
            t, n = chunks[i]
            ti = hf * 2 + t
            s4 = (pc + i) % 4
            cx.dma("sp", xi[s4], xT[n * 128:(n + 1) * 128, 2 + ti * 512:2 + (ti + 1) * 512], [], [xib[s4]], "xi%d" % s4)

        for i in range(3):
            load_x(i)
        for t in range(2):
            rstd_from_ss(cx, rt[t], cx.ps[4 + t][:], 512, D, [cx.psb[4 + t]], rtb[t])
        for i, (t, n) in enumerate(chunks):
            ti = hf * 2 + t
            if i + 3 < len(chunks):
                load_x(i + 3)
            s4 = (pc + i) % 4
            o2 = (pc + i) % n_ot
            mo = MOf[:, (t * KC + n) * 512:(t * KC + n + 1) * 512]
            cx.stt(mo, mo, nw2[:, n:n + 1], rt[t], ALU.mult, ALU.mult, [mobuf[t][n], rtb[t], nw2b], [mobuf[t][n]])
            cx.tt("dve", ot[o2], mo, xi[s4], ALU.add, [mobuf[t][n], xib[s4]], [otb[o2]])
            cx.dma("sp", out[n * 128:(n + 1) * 128, ti * 512:(ti + 1) * 512], ot[o2], [otb[o2]], [], "ot%d" % o2, final=True)
        pc += len(chunks)


def halo_ranges(ntw):
    r = [(0, 2)]
    c = 2
    while c < T + 2:
        r.append((c, ntw))
        c += ntw
    return r


def plain_ranges(ntw, total=T):
    return [(c, ntw) for c in range(0, total, ntw)]


TILES_H = [(0, 2)] + [(2 + i * 512, 512) for i in range(4)]
TILES_P = [(i * 512, 512) for i in range(4)]


def tile_of_h(c0):
    return 0 if c0 < 2 else 1 + (c0 - 2) // 512


def post_norm_residual(cx, ss, ssb, moS, mob, nw2, xT_cols, outT_cols, tagp):
    mi, mib, xi, xib, ot, otb, nr_rt, nr_rtb = tagp
    rstd_from_ss(cx, nr_rt[:, :512], ss[:, :512], 512, D, [ssb], nr_rtb)
    for n in range(KC):
        s = n % 3
        cx.dma("sp", mi[s][:, :], moS[n], [mob[n]], [mib[s]], "mi%d" % s)
        cx.dma("sp", xi[s][:, :], xT_cols(n), [], [xib[s]], "xi%d" % s)
        cx.stt(mi[s][:, :], mi[s][:, :], nw2[:, n:n + 1], nr_rt[:, :512], ALU.mult, ALU.mult, [mib[s], nr_rtb], [mib[s]])
        cx.tt("dve", ot[s][:, :], mi[s][:, :], xi[s][:, :], ALU.add, [mib[s], xib[s]], [otb[s]])
        cx.dma("sp", outT_cols(n), ot[s][:, :], [otb[s]], [], "ot%d" % s, final=True)


def alloc_post(cx):
    mi = [cx.sb([128, 512], F32, "mi%d" % i) for i in range(3)]
    xi = [cx.sb([128, 512], F32, "xi%d" % i) for i in range(3)]
    ot = [cx.sb([128, 512], F32, "ot%d" % i) for i in range(3)]
    rt = cx.sb([128, 512], F32, "post_rt")
    return (mi, [Buf() for _ in range(3)], xi, [Buf() for _ in range(3)], ot, [Buf() for _ in range(3)], rt, Buf())


def build_C():
    cx = Cx()
    nc = cx.nc
    xT = cx.din("xT", [D, T + 2])
    nw1_d = cx.din("nw1", [128, KC])
    nw2_d = cx.din("nw2", [128, KC])
    wup = cx.din("wup", [D, 2 * FFD])
    cw_d = cx.din("cw", [128, FC, 3])
    cb_d = cx.din("cb", [128, FC])
    wdn = cx.din("wdn", [FFD, D])
    xo = cx.dout("xoT", [D, T])
    actS = cx.dscr("actS", [FC, 128, T], BF16)
    actSb = [Buf() for _ in range(FC)]

    nw1, nw1b = cx.load_const(nw1_d, [128, KC], name="nw1")
    nw2, nw2b = cx.load_const(nw2_d, [128, KC], name="nw2")
    cw, cwb = cx.load_const(cw_d, [128, FC, 3], name="cw")
    cbs, cbb = cx.load_const(cb_d, [128, FC], name="cbs")
    cx.arena_init(205 * 1024)

    class ASlot:
        pass

    def aslots(n, kch, name):
        r = []
        for i in range(n):
            a = ASlot()
            a.t = cx.av([128, kch, 128], BF16)
            a.b = Buf()
            a.stream = "w_%s%d" % (name, i)
            r.append(a)
        return r
    wg = aslots(2, KC, "wg")
    wvs = aslots(2, KC, "wv")

    h = cx.av([128, KC, T + 2], BF16)
    hb = [[Buf() for t in range(5)] for k in range(KC)]

    nr = make_nr(cx, cx.av)
    norm_stage(cx, nr, xT, nw1, nw1b, h, lambda k, c0: hb[k][tile_of_h(c0)], halo_ranges(nr.ntw))

    wup_v = wup.rearrange("(k p) n -> p k n", p=128)
    UG = [cx.av([128, T + 2], F32) for _ in range(2)]
    ugb = [[Buf() for _ in range(5)] for _ in range(2)]
    CT = [cx.av([128, T], F32) for _ in range(2)]
    ctb = [Buf(), Buf()]
    GA = [cx.av([128, T], F32) for _ in range(2)]
    gab = [Buf(), Buf()]
    AB = [cx.av([128, T], BF16) for i in range(2)]
    abb = [Buf() for _ in range(2)]
    gws = wstream(cx, wg, [wup_v[:, :, j * 128:(j + 1) * 128] for j in range(FC)])
    vws = wstream(cx, wvs, [wup_v[:, :, FFD + j * 128:FFD + (j + 1) * 128] for j in range(FC)])
    bk = [0]

    def bank():
        b = bk[0] % 6
        bk[0] += 1
        return b

    def up_g(j):
        s = next(gws)
        p = j % 2
        for ti, (c0, w) in enumerate(TILES_H):
            b = bank()
            for k in range(KC):
                cx.mm(cx.ps[b][:, :w], s.t[:, k, :], h[:, k, c0:c0 + w], k == 0, k == KC - 1, [s.b, hb[k][ti]], [cx.psb[b]])
            cx.copy("act", UG[p][:, c0:c0 + w], cx.ps[b][:, :w], [cx.psb[b]], [ugb[p][ti]])
        cx.ts("dve", CT[p], UG[p][:, 0:T], cw[:, j, 0:1], None, ALU.mult, None, ugb[p] + [cwb], [ctb[p]])
        cx.stt(CT[p], UG[p][:, 1:T + 1], cw[:, j, 1:2], CT[p], ALU.mult, ALU.add, ugb[p] + [cwb, ctb[p]], [ctb[p]])
        cx.stt(CT[p], UG[p][:, 2:T + 2], cw[:, j, 2:3], CT[p], ALU.mult, ALU.add, ugb[p] + [cwb, ctb[p]], [ctb[p]])
        cx.act(GA[p], CT[p], AF.Gelu_apprx_tanh, [ctb[p], cbb], [gab[p]], bias=cbs[:, j:j + 1])

    def up_v(j):
        s = next(vws)
        p = j % 2
        for ti in range(1, 5):
            c0, w = TILES_H[ti]
            b = bank()
            for k in range(KC):
                cx.mm(cx.ps[b][:, :w], s.t[:, k, :], h[:, k, c0:c0 + w], k == 0, k == KC - 1, [s.b, hb[k][ti]], [cx.psb[b]])
            cx.tt("dve", AB[p][:, c0 - 2:c0 - 2 + w], GA[p][:, c0 - 2:c0 - 2 + w], cx.ps[b][:, :w], ALU.mult, [gab[p], cx.psb[b]], [abb[p]])
        cx.dma("sp", actS[j], AB[p], [abb[p]], [actSb[j]], "ab%d" % p)

    up_g(0)
    for j in range(FC):
        if j + 1 < FC:
            up_g(j + 1)
        up_v(j)

    cx.stage()
    ACT_ = cx.av([128, FC, 1024], BF16)
    atb = [Buf() for _ in range(4)]
    wdn_v = wdn.rearrange("(k p) n -> p k n", p=128)
    wd = aslots(2, FC, "wd")
    actS_v = actS.rearrange("j p t -> p j t")
    dws = wstream(cx, wd, [wdn_v[:, :, n * 128:(n + 1) * 128] for _ in range(2) for n in range(KC)])

    def load_half(hf):
        for jg in range(4):
            cx.dma("sp", ACT_[:, jg * 11:(jg + 1) * 11, :], actS_v[:, jg * 11:(jg + 1) * 11, hf * 1024:(hf + 1) * 1024],
                   actSb[jg * 11:(jg + 1) * 11], [atb[jg]], "at%d" % jg)
    down_and_post(cx, FC, dws, lambda j, ti, t: ACT_[:, j, t * 512:(t + 1) * 512], lambda j: [atb[j // 11]],
                  nw2, nw2b, xT, xo, cx.av, load_half)
    cx.P.emit()
    return nc


class Slot:
    def __init__(self, cx, kch, name):
        self.t = cx.sb([128, kch, 128], BF16, name)
        self.b = Buf(name)
        self.stream = "w_" + name


def wstream(cx, slots, loads):
    depth = len(slots) - 1

    def issue(i):
        s = slots[i % len(slots)]
        kch = loads[i].shape[1]
        cx.dma("pool", s.t[:, :kch, :], loads[i], [], [s.b], s.stream)

    for i in range(min(depth, len(loads))):
        issue(i)
    for i in range(len(loads)):
        if i + depth < len(loads):
            issue(i + depth)
        yield slots[i % len(slots)]


class ASlot:
    pass


def aslot(cx, kch, name):
    a = ASlot()
    a.t = cx.av([128, kch, 128], BF16)
    a.b = Buf()
    a.stream = "w_" + name
    return a


def build_A(layer):
    cx = Cx()
    nc = cx.nc
    xT = cx.din("xT", [D, T])
    nw_d = cx.din("nw", [128, KC])
    w_in = cx.din("w_in", [D, NIN])
    lbz_d = cx.din("lbz", [128, 8, 2])
    ident_d = cx.din("ident", [128, 128])
    bm_d = cx.din("bmask", [128, 512])
    rmask_d = cx.din("rmask", [128, T])
    oloc = cx.dout("oloc", [8, 128, T])
    qBo = cx.dout("qB", [8, 128, T], BF16)
    Uc = cx.dout("Ucore", [128, 8, 128])
    Fc = cx.dout("Fcore", [128, 8])

    nw, nwb = cx.load_const(nw_d, [128, KC], name="nw")
    lbz, lbzb = cx.load_const(lbz_d, [128, 8, 2], name="lbz")
    BM, bmb = cx.load_const(bm_d, [128, 512], name="BM")
    ident = cx.sb([128, 128], BF16, "ident")
    idb = Buf()
    cx.dma("pool", ident[:], ident_d, [], [idb], "c_ident")
    rmask = cx.sb([128, T], BF16, "rmask")
    rmb = Buf()
    cx.dma("pool", rmask[:], rmask_d, [], [rmb], "c_rmask")
    onesT = cx.sb([128, T], BF16, "onesT")
    otb = Buf()
    cx.memset("pool", onesT[:], 1.0, [otb])

    sm = cx.sb([128, 8, 8], F32, "sm")
    smb = Buf()
    cx.tt("dve", sm[:, 0, :], lbz[:, :, 0], lbz[:, :, 1], ALU.max, [lbzb], [smb])
    for l in range(2):
        cx.tt("dve", sm[:, 1 + l, :], lbz[:, :, l], sm[:, 0, :], ALU.subtract, [lbzb, smb], [smb])
        cx.act(sm[:, 1 + l, :], sm[:, 1 + l, :], AF.Exp, [smb], [smb])
    cx.tt("dve", sm[:, 3, :], sm[:, 1, :], sm[:, 2, :], ALU.add, [smb], [smb])
    cx.recip(sm[:, 3, :], sm[:, 3, :], [smb], [smb])
    for l in range(2):
        cx.tt("dve", sm[:, 1 + l, :], sm[:, 1 + l, :], sm[:, 3, :], ALU.mult, [smb], [smb])
    cx.copy("dve", sm[:, 4, :], sm[:, 1, :], [smb], [smb])
    for l in range(1, layer + 1):
        cx.tt("dve", sm[:, 4, :], sm[:, 4, :], sm[:, 1 + l, :], ALU.add, [smb], [smb])
    cx.tt("dve", sm[:, 4, :], sm[:, 4, :], sm[:, 1, :], ALU.subtract, [smb], [smb])
    cx.ts("dve", sm[:, 4, :], sm[:, 4, :], 0.0, 1.0, ALU.max, ALU.min, [smb], [smb])
    cx.ts("dve", sm[:, 5, :], sm[:, 4, :], -1.0, 1.0, ALU.mult, ALU.add, [smb], [smb])
    lb = lambda hd: sm[:, 4, hd:hd + 1]
    oml = lambda hd: sm[:, 5, hd:hd + 1]

    h = cx.sb([128, KC, T], BF16, "h")
    hb = [[Buf() for t in range(4)] for k in range(KC)]
    nr = make_nr(cx, lambda sh, dt: cx.sb(sh, dt))
    norm_stage(cx, nr, xT, nw, nwb, h, lambda k, c0: hb[k][c0 // 512], plain_ranges(nr.ntw))

    Bt = [cx.sb([128, T], F32, "B%d" % i) for i in range(5)]
    Bb = [[Buf() for t in range(4)] for i in range(5)]
    KE = cx.sb([128, T], BF16, "KE"); keb = Buf()
    VT = cx.sb([128, T], BF16, "VT"); vtb = Buf()
    QE = cx.sb([128, T], BF16, "QE"); qeb = Buf()
    QB = cx.sb([128, T], BF16, "QB"); qbb = Buf()
    KT = [cx.sb([128, T], BF16, "KT%d" % i) for i in range(2)]; ktb = [Buf(), Buf()]
    VK = cx.sb([128, T], BF16, "VK"); vkb = Buf()
    PT = cx.sb([128, T], BF16, "PT"); ptb = [Buf() for _ in range(4)]
    UD = [cx.sb([128, 128], F32, "UD%d" % i) for i in range(4)]; udb = [Buf() for _ in range(4)]
    Sf = [cx.sb([128, 128], F32, "Sf%d" % i) for i in range(2)]; sfb = [Buf(), Buf()]
    Sb = [cx.sb([128, 128], BF16, "Sb%d" % i) for i in range(2)]; sbb = [Buf(), Buf()]
    UC = cx.sb([128, 8, 128], F32, "UC"); ucb = Buf()
    udab = [Buf() for _ in range(32)]
    FCt = cx.sb([128, 8], F32, "FCt"); fcb = Buf()
    pm = cx.sb([128, 2], F32, "pm"); pmb = Buf()
    cx.memset("dve", pm[:, :], 0.0, [pmb])
    cx.memset("dve", pm[0:64, 0:1], 1.0, [pmb])
    cx.memset("dve", pm[64:128, 1:2], 1.0, [pmb])

    wv_ = w_in.rearrange("(k p) n -> p k n", p=128)
    slots = [Slot(cx, KC, "ws%d" % i) for i in range(3)]
    loads = []
    for hd in range(8):
        for base in (4096, 5120, 3072):
            loads.append(wv_[:, :, base + hd * 128:base + (hd + 1) * 128])
    ws = wstream(cx, slots, loads)
    qscale = 128.0 ** -0.5
    allB = lambda i: Bb[i]
    for hd in range(8):
        s = next(ws)
        for ti, (c0, w) in enumerate(TILES_P):
            for k in range(KC):
                cx.mm(cx.ps[ti][:], s.t[:, k, :], h[:, k, c0:c0 + w], k == 0, k == KC - 1, [s.b, hb[k][ti]], [cx.psb[ti]])
            sl = slice(c0, c0 + w)
            cx.act(Bt[0][:, sl], cx.ps[ti][:], AF.Sigmoid, [cx.psb[ti]], [Bb[0][ti]])
            cx.ts("dve", Bt[0][:, sl], Bt[0][:, sl], oml(hd), lb(hd), ALU.mult, ALU.add, [Bb[0][ti], smb], [Bb[0][ti]])
            cx.act(Bt[1][:, sl], Bt[0][:, sl], AF.Ln, [Bb[0][ti]], [Bb[1][ti]])
            cx.ts("dve", Bt[0][:, sl], Bt[0][:, sl], -1.0, 1.0, ALU.mult, ALU.add, [Bb[0][ti]], [Bb[0][ti]])
        cx.scan(Bt[2][:], rmask[:], Bt[1][:], allB(1) + [rmb], allB(2))
        cx.scan(Bt[3][:], onesT[:], Bt[1][:], allB(1) + [otb], allB(3))
        cx.act(Bt[1][:], Bt[2][:], AF.Exp, allB(2), allB(1))
        cx.act(Bt[2][:], Bt[2][:], AF.Exp, allB(2), allB(2), scale=-1.0)
        cx.act(Bt[3][:], Bt[3][:], AF.Exp, allB(3), allB(3))
        cx.tt("dve", KE[:], Bt[0][:], Bt[2][:], ALU.mult, allB(0) + allB(2), [keb])
        s = next(ws)
        for ti, (c0, w) in enumerate(TILES_P):
            for k in range(KC):
                cx.mm(cx.ps[ti][:], s.t[:, k, :], h[:, k, c0:c0 + w], k == 0, k == KC - 1, [s.b, hb[k][ti]], [cx.psb[ti]])
            cx.copy("act", VT[:, c0:c0 + w], cx.ps[ti][:], [cx.psb[ti]], [vtb])
        s = next(ws)
        for ti, (c0, w) in enumerate(TILES_P):
            for k in range(KC):
                cx.mm(cx.ps[ti][:], s.t[:, k, :], h[:, k, c0:c0 + w], k == 0, k == KC - 1, [s.b, hb[k][ti]], [cx.psb[ti]])
            cx.act(Bt[4][:, c0:c0 + w], cx.ps[ti][:], AF.Silu, [cx.psb[ti]], [Bb[4][ti]])
        cx.stt(QE[:], Bt[4][:], qscale, Bt[1][:], ALU.mult, ALU.mult, allB(4) + allB(1), [qeb])
        cx.stt(QB[:], Bt[4][:], qscale, Bt[3][:], ALU.mult, ALU.mult, allB(4) + allB(3), [qbb])
        cx.dma("sp", qBo[hd], QB[:], [qbb], [], "o_qb", final=True)
        for half in range(2):
            for j in range(8):
                blk = half * 8 + j
                cx.tr(cx.pst[:, j * 128:(j + 1) * 128], KE[:, blk * 128:(blk + 1) * 128], ident[:], [keb, idb], [cx.pstb])
            hs = slice(half * 1024, (half + 1) * 1024)
            cx.ts("dve", KT[0][:, hs], cx.pst[:], pm[:, 0:1], None, ALU.mult, None, [cx.pstb, pmb], [ktb[0]])
            cx.ts("dve", KT[1][:, hs], cx.pst[:], pm[:, 1:2], None, ALU.mult, None, [cx.pstb, pmb], [ktb[1]])
            for j in range(8):
                blk = half * 8 + j
                cx.tr(cx.pst[:, j * 128:(j + 1) * 128], VT[:, blk * 128:(blk + 1) * 128], ident[:], [vtb, idb], [cx.pstb])
            cx.copy("dve", VK[:, hs], cx.pst[:], [cx.pstb], [vkb])
        for blk in range(16):
            b = 4 + (blk // 4) % 2
            cs = slice((blk % 4) * 128, (blk % 4 + 1) * 128)
            bs = slice(blk * 128, (blk + 1) * 128)
            cx.mm(cx.ps[b][:, cs], KE[:, bs], QE[:, bs], True, True, [keb, qeb], [cx.psb[b]])
            if blk % 4 == 3:
                q4 = blk // 4
                cx.tt("dve", PT[:, q4 * 512:(q4 + 1) * 512], cx.ps[b][:], BM[:], ALU.mult, [cx.psb[b], bmb], [ptb[q4]])
        UDa = nr.xt[0]
        for c in range(32):
            blk = c // 2
            bs = slice(blk * 128, (blk + 1) * 128)
            ub = c % 7
            cx.mm(cx.ps[ub][:, 0:128], KT[c % 2][:, bs], VK[:, bs], True, True, [ktb[c % 2], vkb], [cx.psb[ub]])
            dcol = Bt[1][:, c * 64 + 63:c * 64 + 64]
            cx.ts("dve", UDa[:, c // 2, (c % 2) * 128:(c % 2 + 1) * 128], cx.ps[ub][:, 0:128], dcol, None, ALU.mult, None,
                  [cx.psb[ub]] + allB(1), [udab[c]])
        cx.memset("dve", Sf[0][:], 0.0, [sfb[0]])
        cx.memset("dve", Sb[0][:], 0.0, [sbb[0]])
        for c in range(32):
            blk = c // 2
            bs = slice(blk * 128, (blk + 1) * 128)
            cs = slice(c * 64, (c + 1) * 64)
            cur, nxt = c % 2, (c + 1) % 2
            ob = blk // 4
            oc = (blk % 4) * 128
            if c % 2 == 0:
                cx.mm(cx.ps[ob][:, oc:oc + 128], VK[:, bs], PT[:, bs], True, False, [vkb, ptb[blk // 4]], [cx.psb[ob]])
            cx.mm(cx.ps[ob][:, oc + (c % 2) * 64:oc + (c % 2) * 64 + 64], Sb[cur][:], QE[:, cs], False, c % 2 == 1,
                  [sbb[cur], qeb], [cx.psb[ob]])
            dcol = Bt[1][:, c * 64 + 63:c * 64 + 64]
            cx.stt(Sf[nxt][:], Sf[cur][:], dcol, UDa[:, c // 2, (c % 2) * 128:(c % 2 + 1) * 128], ALU.mult, ALU.add,
                   [sfb[cur], udab[c]] + allB(1), [sfb[nxt]])
            cx.copy("act", Sb[nxt][:], Sf[nxt][:], [sfb[nxt]], [sbb[nxt]])
            if c % 8 == 7:
                cx.copy("act", Bt[4][:, ob * 512:(ob + 1) * 512], cx.ps[ob][:], [cx.psb[ob]], [Bb[4][ob]])
        cx.dma("sp", oloc[hd], Bt[4][:], allB(4), [], "o_ol", final=True)
        cx.copy("dve", UC[:, hd, :], Sf[0][:], [sfb[0]], [ucb])
        cx.copy("dve", FCt[:, hd:hd + 1], Bt[3][:, T - 1:T], allB(3), [fcb])
    cx.dma("sp", Uc, UC[:], [ucb], [], "o_uc", final=True)
    cx.dma("sp", Fc, FCt[:], [fcb], [], "o_fc", final=True)
    cx.P.emit()
    return nc


def build_B():
    cx = Cx()
    nc = cx.nc
    xT = cx.din("xT", [D, T + 2])
    nw_d = cx.din("nw", [128, KC])
    nwp_d = cx.din("nwp", [128, KC])
    nwm_d = cx.din("nwm", [128, KC])
    w_in = cx.din("w_in", [D, NIN])
    cmw_d = cx.din("cmw", [128, 8, 3])
    hnw_d = cx.din("hnw", [128, 8])
    memT = cx.din("memT", [D, 256])
    wkv = cx.din("wkv", [D, 2048])
    wbr = cx.din("wbr", [3, 1024, D])
    wout = cx.din("wout", [D, D])
    oloc = cx.din("oloc", [8, 128, T])
    qBi = cx.din("qB", [8, 128, T], BF16)
    Uall = cx.din("Uall", [7, 128, 8, 128])
    Fall = cx.din("Fall", [128, 7, 8])
    ident_d = cx.din("ident", [128, 128])
    xm = cx.dout("xmT", [D, T])
    mergedS = cx.dscr("mergedS", [KC, 128, T], BF16)
    Pd = cx.dscr("Pd", [KC, 128, T], F32)
    mgb = [[Buf() for _ in range(4)] for _ in range(KC)]
    pdb = [[Buf() for _ in range(4)] for _ in range(KC)]

    nw, nwb = cx.load_const(nw_d, [128, KC], name="nw")
    nwp, nwpb = cx.load_const(nwp_d, [128, KC], name="nwp")
    nwm, nwmb = cx.load_const(nwm_d, [128, KC], name="nwm")
    cmw, cmwb = cx.load_const(cmw_d, [128, 8, 3], name="cmw")
    hnw, hnwb = cx.load_const(hnw_d, [128, 8], name="hnw")
    FA, fab = cx.load_const(Fall, [128, 7, 8], name="FA")
    ident = cx.sb([128, 128], BF16, "ident")
    idb = Buf()
    cx.dma("pool", ident[:], ident_d, [], [idb], "c_ident")

    BIG = cx.sb([128, KC * (T + 2)], BF16, "BIG")
    h = BIG.reshape([128, KC, T + 2])
    hb = [[Buf() for t in range(5)] for k in range(KC)]
    allh = [hb[k][t] for k in range(KC) for t in range(5)]
    slots = [Slot(cx, KC, "ws%d" % i) for i in range(2)]
    bslots = [Slot(cx, 8, "wb%d" % i) for i in range(2)]
    cx.arena_init(130 * 1024)
    Y0 = cx.av([128, 8, T], BF16); y0b = [[Buf() for _ in range(4)] for _ in range(8)]
    Y1 = cx.av([128, 8, T], BF16); y1b = [[Buf() for _ in range(4)] for _ in range(8)]
    base_y = cx.aoff
    MH = cx.av([128, KC, 256], BF16)
    mhb = [Buf() for _ in range(KC)]
    cx.abase = cx.aoff
    nr = make_nr(cx, cx.av)
    norm_stage(cx, nr, xT, nw, nwb, h, lambda k, c0: hb[k][tile_of_h(c0)], halo_ranges(nr.ntw))
    norm_stage(cx, nr, memT, nwm, nwmb, MH, lambda k, c0: mhb[k], [(0, 256)])

    wv_ = w_in.rearrange("(k p) n -> p k n", p=128)
    wkv_v = wkv.rearrange("(k p) n -> p k n", p=128)
    wbr_v = [wbr[br].rearrange("(k p) n -> p k n", p=128) for br in range(3)]
    bk = [0]

    def bank():
        b = bk[0] % 6
        bk[0] += 1
        return b

    def branch_merge(br, Yt, Ybf, first, last):
        SGt = [cx.av([128, 512], F32) for _ in range(2)]; sgtb = [Buf(), Buf()]
        Mx = [cx.av([128, 512], F32) for _ in range(2)]; mxb = [Buf(), Buf()]
        PL = [cx.av([128, 512], F32) for _ in range(2)]; plb = [Buf(), Buf()]
        MGo = [cx.av([128, 512], BF16) for _ in range(2)]; mgob = [Buf(), Buf()]
        gl = [wv_[:, :, 8192 + br * 2048 + n * 128:8192 + br * 2048 + (n + 1) * 128] for n in range(KC)]
        bl = [wbr_v[br][:, :, n * 128:(n + 1) * 128] for n in range(KC)]
        gws = wstream(cx, slots, gl)
        bws = wstream(cx, bslots, bl)
        cnt = 0
        for n in range(KC):
            gs = next(gws)
            bs_ = next(bws)
            for ti in range(4):
                c0, w = TILES_H[ti + 1]
                tsl = slice(ti * 512, (ti + 1) * 512)
                i2 = cnt % 2
                cnt += 1
                bg, bp = bank(), bank()
                for k in range(KC):
                    cx.mm(cx.ps[bg][:], gs.t[:, k, :], h[:, k, c0:c0 + w], k == 0, k == KC - 1, [gs.b, hb[k][ti + 1]], [cx.psb[bg]])
                for k in range(8):
                    cx.mm(cx.ps[bp][:], bs_.t[:, k, :], Yt[:, k, tsl], k == 0, k == 7, [bs_.b, Ybf[k][ti]], [cx.psb[bp]])
                cx.act(SGt[i2], cx.ps[bg][:], AF.Sigmoid, [cx.psb[bg]], [sgtb[i2]])
                if not first:
                    cx.dma("sp", PL[i2], Pd[n][:, tsl], [pdb[n][ti]], [plb[i2]], "pl%d" % i2)
                cx.tt("dve", Mx[i2], SGt[i2], cx.ps[bp][:], ALU.mult, [sgtb[i2], cx.psb[bp]], [mxb[i2]])
                if first:
                    cx.dma("sp", Pd[n][:, tsl], Mx[i2], [mxb[i2]], [pdb[n][ti]], "mxo%d" % i2)
                elif not last:
                    cx.tt("dve", Mx[i2], Mx[i2], PL[i2], ALU.add, [mxb[i2], plb[i2]], [mxb[i2]])
                    cx.dma("sp", Pd[n][:, tsl], Mx[i2], [mxb[i2]], [pdb[n][ti]], "mxo%d" % i2)
                else:
                    cx.tt("dve", MGo[i2], Mx[i2], PL[i2], ALU.add, [mxb[i2], plb[i2]], [mgob[i2]])
                    cx.dma("sp", mergedS[n][:, tsl], MGo[i2], [mgob[i2]], [mgb[n][ti]], "mgo%d" % i2)

    cx.stage()
    MK = cx.av([128, 8, 256], BF16); mkb = Buf()
    MV = cx.av([128, 2, 1024], BF16); mvb = Buf()
    MQ, mqb = Y0, y0b
    YM, ymb = Y1, y1b
    loads = [wkv_v[:, :, n * 128:(n + 1) * 128] for n in range(16)] + [wv_[:, :, 7168 + n * 128:7168 + (n + 1) * 128] for n in range(8)]
    ws = wstream(cx, slots, loads)
    for n in range(8):
        s = next(ws)
        b = bank()
        for k in range(KC):
            cx.mm(cx.ps[b][:, :256], s.t[:, k, :], MH[:, k, :], k == 0, k == KC - 1, [s.b, mhb[k]], [cx.psb[b]])
        cx.copy("act", MK[:, n, :], cx.ps[b][:, :256], [cx.psb[b]], [mkb])
    for n in range(8):
        s = next(ws)
        for mb in range(2):
            b = bank()
            for k in range(KC):
                cx.mm(cx.ps[b][:, :128], MH[:, k, mb * 128:(mb + 1) * 128], s.t[:, k, :], k == 0, k == KC - 1, [s.b, mhb[k]], [cx.psb[b]])
            cx.copy("act", MV[:, mb, n * 128:(n + 1) * 128], cx.ps[b][:, :128], [cx.psb[b]], [mvb])
    for n in range(8):
        s = next(ws)
        for ti in range(1, 5):
            c0, w = TILES_H[ti]
            b = bank()
            for k in range(KC):
                cx.mm(cx.ps[b][:], s.t[:, k, :], h[:, k, c0:c0 + w], k == 0, k == KC - 1, [s.b, hb[k][ti]], [cx.psb[b]])
            cx.act(MQ[:, n, c0 - 2:c0 - 2 + w], cx.ps[b][:], AF.Copy, [cx.psb[b]], [mqb[n][ti - 1]], scale=256.0 ** -0.5)
    PTT = cx.av([128, 2, T], BF16); pttb = [Buf() for _ in range(4)]
    PE_ = [cx.av([128, 256], F32) for i in range(4)]; peb = [Buf() for _ in range(4)]
    PN = [cx.av([128, 256], BF16) for i in range(4)]; pnb = [Buf() for _ in range(4)]
    st = [cx.av([128, 4], F32) for i in range(4)]; stb = [Buf() for _ in range(4)]
    pstq = [Buf() for _ in range(4)]
    blocks = [(a, sbk) for a in range(4) for sbk in range(16)]

    TB = [cx.ps[4].bitcast(BF16), cx.ps[5].bitcast(BF16), cx.ps[6].bitcast(BF16), cx.pst]
    tbb = [cx.psb[4], cx.psb[5], cx.psb[6], cx.pstb]
    ab = [0]

    def abank():
        b = ab[0] % 4
        ab[0] += 1
        return b

    def att_front(i):
        a, sbk = blocks[i]
        ti = sbk // 4
        b = abank()
        i2 = i % 4
        ss_ = slice(sbk * 128, (sbk + 1) * 128)
        for dc in range(2):
            cx.mm(cx.ps[b][:, :256], MQ[:, 2 * a + dc, ss_], MK[:, 2 * a + dc, :], dc == 0, dc == 1, [mqb[2 * a + dc][ti], mkb], [cx.psb[b]])
        cx.P.op("dve", lambda e, o=st[i2][:, 0:1], i_=cx.ps[b][:, :256]: e.tensor_reduce(out=o, in_=i_, axis=AX.X, op=ALU.max, negate=True), [cx.psb[b]], [stb[i2]])
        cx.act(PE_[i2], cx.ps[b][:, :256], AF.Exp, [cx.psb[b], stb[i2]], [peb[i2], stb[i2]], bias=st[i2][:, 0:1], accum=st[i2][:, 2:3])
        cx.recip(st[i2][:, 3:4], st[i2][:, 2:3], [stb[i2]], [stb[i2]])
        cx.ts("dve", PN[i2], PE_[i2], st[i2][:, 3:4], None, ALU.mult, None, [peb[i2], stb[i2]], [pnb[i2]])

    def att_back(i):
        a, sbk = blocks[i]
        i2 = i % 4
        q4 = i % 4
        ss_ = slice(sbk * 128, (sbk + 1) * 128)
        for mb in range(2):
            cx.tr(TB[q4][:, mb * 128:(mb + 1) * 128], PN[i2][:, mb * 128:(mb + 1) * 128], ident[:], [pnb[i2], idb], [tbb[q4]])
        eng = "act" if i % 2 == 0 else "dve"
        cx.copy(eng, PTT[:, :, ss_], TB[q4][:, 0:256].rearrange("p (m s) -> p m s", s=128), [tbb[q4]], [pttb0[sbk], pttb1[sbk]])

    def att_pv(a):
        for dc in range(2):
            for ti in range(4):
                b = abank()
                tsl = slice(ti * 512, (ti + 1) * 512)
                for mb in range(2):
                    cx.mm(cx.ps[b][:], MV[:, mb, (2 * a + dc) * 128:(2 * a + dc + 1) * 128], PTT[:, mb, tsl], mb == 0, mb == 1,
                          [mvb] + pttb0[ti * 4:ti * 4 + 4] + pttb1[ti * 4:ti * 4 + 4], [cx.psb[b]])
                cx.copy("act", YM[:, 2 * a + dc, tsl], cx.ps[b][:], [cx.psb[b]], [ymb[2 * a + dc][ti]])

    pttb0 = [Buf() for _ in range(16)]
    pttb1 = [Buf() for _ in range(16)]
    LAG = 2
    for i in range(len(blocks) + LAG):
        if i < len(blocks):
            att_front(i)
        j = i - LAG
        if j >= 0:
            att_back(j)
            if j % 16 == 15:
                att_pv(j // 16)
    branch_merge(2, YM, ymb, True, False)

    cx.abase = base_y
    cx.stage()
    YA, yab = Y0, y0b
    CC = cx.av([128, T + 2], F32); ccb = [Buf() for _ in range(5)]
    UU = cx.av([128, T + 2], F32); uub = [Buf() for _ in range(5)]
    CT = cx.av([128, T], F32); ctb = Buf()
    loads = []
    for j in range(8):
        for base in (1024, 2048, 0):
            loads.append(wv_[:, :, base + j * 128:base + (j + 1) * 128])
    ws = wstream(cx, slots, loads)
    for j in range(8):
        s = next(ws)
        for ti, (c0, w) in enumerate(TILES_H):
            b = bank()
            for k in range(KC):
                cx.mm(cx.ps[b][:, :w], s.t[:, k, :], h[:, k, c0:c0 + w], k == 0, k == KC - 1, [s.b, hb[k][ti]], [cx.psb[b]])
            cx.copy("act", CC[:, c0:c0 + w], cx.ps[b][:, :w], [cx.psb[b]], [ccb[ti]])
        s = next(ws)
        for ti, (c0, w) in enumerate(TILES_H):
            b = bank()
            for k in range(KC):
                cx.mm(cx.ps[b][:, :w], s.t[:, k, :], h[:, k, c0:c0 + w], k == 0, k == KC - 1, [s.b, hb[k][ti]], [cx.psb[b]])
            cx.tt("dve", UU[:, c0:c0 + w], CC[:, c0:c0 + w], cx.ps[b][:, :w], ALU.mult, [ccb[ti], cx.psb[b]], [uub[ti]])
        cx.ts("dve", CT, UU[:, 0:T], cmw[:, j, 0:1], None, ALU.mult, None, uub + [cmwb], [ctb])
        cx.stt(CT, UU[:, 1:T + 1], cmw[:, j, 1:2], CT, ALU.mult, ALU.add, uub + [cmwb, ctb], [ctb])
        cx.stt(CT, UU[:, 2:T + 2], cmw[:, j, 2:3], CT, ALU.mult, ALU.add, uub + [cmwb, ctb], [ctb])
        s = next(ws)
        for ti in range(1, 5):
            c0, w = TILES_H[ti]
            b = bank()
            for k in range(KC):
                cx.mm(cx.ps[b][:], s.t[:, k, :], h[:, k, c0:c0 + w], k == 0, k == KC - 1, [s.b, hb[k][ti]], [cx.psb[b]])
            cx.tt("dve", YA[:, j, c0 - 2:c0 - 2 + w], CT[:, c0 - 2:c0 - 2 + w], cx.ps[b][:], ALU.mult, [ctb, cx.psb[b]], [yab[j][ti - 1]])
    branch_merge(0, YA, yab, False, False)

    cx.stage()
    YB, ybb = Y1, y1b
    UA = [cx.av([128, 7, 128], F32) for _ in range(2)]; uab = [Buf(), Buf()]
    Pst = [cx.av([128, 128], F32) for _ in range(2)]; pstb_ = [Buf(), Buf()]
    SSb = [cx.av([128, 128], BF16) for _ in range(2)]; ssbb = [Buf(), Buf()]
    OL = cx.av([128, T], F32); olb = [Buf() for _ in range(4)]
    QBt = [cx.av([128, T], BF16) for _ in range(2)]; qbtb = [Buf(), Buf()]
    SQo = [cx.av([128, 512], BF16) for i in range(2)]; sqob = [Buf(), Buf()]
    RT = [cx.av([128, 512], F32) for i in range(2)]; rtb = [Buf(), Buf()]
    SG4 = [[cx.av([128, 512], F32) for t in range(4)] for p in range(2)]
    sg4b = [[Buf() for t in range(4)] for p in range(2)]
    TMP = [cx.av([128, 512], F32) for i in range(2)]; tmpb = [Buf(), Buf()]
    loads = [wv_[:, :, 6144 + hd * 128:6144 + (hd + 1) * 128] for hd in range(8)]
    ws = wstream(cx, slots, loads)
    b4 = [0]

    def hg_proj(hd):
        s = next(ws)
        p = hd % 2
        for ti in range(4):
            c0, w = TILES_H[ti + 1]
            b2 = b4[0] % 4
            b4[0] += 1
            for k in range(KC):
                cx.mm(cx.ps[b2][:], s.t[:, k, :], h[:, k, c0:c0 + w], k == 0, k == KC - 1, [s.b, hb[k][ti + 1]], [cx.psb[b2]])
            cx.act(SG4[p][ti], cx.ps[b2][:], AF.Silu, [cx.psb[b2]], [sg4b[p][ti]])

    def prologue(hd):
        p = hd % 2
        cx.dma("sp", UA[p], Uall[:, :, hd, :].rearrange("r p v -> p r v"), [], [uab[p]], "ua%d" % p)
        cx.dma("sp", QBt[p], qBi[hd], [], [qbtb[p]], "qbt%d" % p)
        cx.memset("dve", Pst[p], 0.0, [pstb_[p]])
        for i in range(7):
            cx.stt(Pst[p], Pst[p], FA[:, i, hd:hd + 1], UA[p][:, i, :], ALU.mult, ALU.add, [pstb_[p], fab, uab[p]], [pstb_[p]])
        cx.copy("act", SSb[p], Pst[p], [pstb_[p]], [ssbb[p]])

    hg_proj(0)
    prologue(0)
    cnt = 0
    for hd in range(8):
        p = hd % 2
        for ti in range(4):
            cx.dma("sp", OL[:, ti * 512:(ti + 1) * 512], oloc[hd][:, ti * 512:(ti + 1) * 512], [], [olb[ti]], "ol%d" % ti)
        if hd + 1 < 8:
            hg_proj(hd + 1)
            prologue(hd + 1)
        for ti in range(4):
            tsl = slice(ti * 512, (ti + 1) * 512)
            i2 = cnt % 2
            cnt += 1
            b = 4 + i2
            cx.mm(cx.ps[b][:], SSb[p], QBt[p][:, tsl], True, True, [ssbb[p], qbtb[p]], [cx.psb[b]])
            cx.tt("dve", OL[:, tsl], OL[:, tsl], cx.ps[b][:], ALU.add, [olb[ti], cx.psb[b]], [olb[ti]])
            cx.act(SQo[i2], OL[:, tsl], AF.Square, [olb[ti]], [sqob[i2]])
            cx.mm(cx.ps[6][:], cx.ones[:], SQo[i2], True, True, [sqob[i2], cx.onesb], [cx.psb[6]])
            rstd_from_ss(cx, RT[i2], cx.ps[6][:], 512, 128, [cx.psb[6]], rtb[i2])
            cx.stt(TMP[i2], OL[:, tsl], hnw[:, hd:hd + 1], RT[i2], ALU.mult, ALU.mult, [olb[ti], hnwb, rtb[i2]], [tmpb[i2]])
            cx.tt("dve", YB[:, hd, tsl], TMP[i2], SG4[p][ti], ALU.mult, [tmpb[i2], sg4b[p][ti]], [ybb[hd][ti]])
    branch_merge(1, YB, ybb, False, True)

    cx.abase = 0
    cx.stage()
    mg_v = mergedS.rearrange("k p t -> p k t")
    wo_v = wout.rearrange("(k p) n -> p k n", p=128)
    MGv = BIG[:, 0:KC * T].rearrange("p (k t) -> p k t", t=T)
    mgk = [Buf() for _ in range(4)]
    for kg in range(4):
        cx.dma("sp", MGv[:, kg * 4:(kg + 1) * 4, :], mg_v[:, kg * 4:(kg + 1) * 4, :],
               [mgb[k][ti] for k in range(kg * 4, kg * 4 + 4) for ti in range(4)], [mgk[kg]] + (allh if kg == 0 else []), "at%d" % kg)
    ws = wstream(cx, slots + [aslot(cx, KC, "s4a"), aslot(cx, KC, "s4b")], [wo_v[:, :, n * 128:(n + 1) * 128] for _ in range(2) for n in range(KC)])
    down_and_post(cx, KC, ws, lambda k, ti, t: MGv[:, k, ti * 512:(ti + 1) * 512], lambda k: [mgk[k // 4]],
                  nwp, nwpb, xT, xm, cx.av, None, 4)
    cx.P.emit()
    return nc


def _lay(v):
    return np.ascontiguousarray(np.asarray(v, np.float32).reshape(-1, 128).T)


def _consts():
    ident = np.eye(128, dtype=np.float32)
    j = np.arange(128)[:, None]
    i = np.arange(128)[None, :]
    bm = ((j // 64 == i // 64) & (j % 64 <= i % 64)).astype(np.float32)
    bmask = np.ascontiguousarray(np.tile(bm, (1, 4)))
    rmask = np.ones((128, T), np.float32)
    rmask[:, ::64] = 0.0
    return ident, bmask, rmask


def a_inputs(inp, x, l, c):
    ident, bmask, rmask = _consts()
    lbz = np.ascontiguousarray(np.asarray(inp["hg_lower_bounds"], np.float32).reshape(2, 8, 128).transpose(2, 1, 0))
    return {"xT": np.ascontiguousarray(x[c * T:(c + 1) * T].T), "nw": _lay(inp["norm_mix_pre"][l]),
            "w_in": np.asarray(inp["w_in"][l]), "lbz": lbz, "ident": ident, "bmask": bmask, "rmask": rmask}


def _with_halo_T(x, c):
    halo = x[c * T - 2:c * T] if c > 0 else np.zeros((2, D), np.float32)
    return np.ascontiguousarray(np.concatenate([halo, x[c * T:(c + 1) * T]], 0).T)


def b_inputs(inp, x, l, c, ares):
    ident, _, _ = _consts()
    Uall = np.zeros((7, 128, 8, 128), np.float32)
    Fall = np.zeros((128, 7, 8), np.float32)
    for i in range(7):
        r = c - 7 + i
        if r >= 0:
            Uall[i] = ares[r]["Ucore"]
            Fall[:, i, :] = ares[r]["Fcore"]
    return {"xT": _with_halo_T(x, c), "nw": _lay(inp["norm_mix_pre"][l]), "nwp": _lay(inp["norm_mix_post"][l]),
            "nwm": _lay(inp["norm_mem"][l]), "w_in": np.asarray(inp["w_in"][l]),
            "cmw": np.ascontiguousarray(np.asarray(inp["conv_mix_w"][l]).reshape(3, 8, 128).transpose(2, 1, 0)),
            "hnw": _lay(inp["hg_norm_w"][l]), "memT": np.ascontiguousarray(np.asarray(inp["mem"])[0].T),
            "wkv": np.asarray(inp["w_mem_kv"][l]), "wbr": np.asarray(inp["w_branch"][l]), "wout": np.asarray(inp["w_out"][l]),
            "oloc": ares[c]["oloc"], "qB": ares[c]["qB"], "Uall": Uall, "Fall": Fall, "ident": ident}


def c_inputs(inp, x, l, c):
    return {"xT": _with_halo_T(x, c), "nw1": _lay(inp["norm_ffn_pre"][l]), "nw2": _lay(inp["norm_ffn_post"][l]),
            "wup": np.asarray(inp["w_ffn_up"][l]),
            "cw": np.ascontiguousarray(np.asarray(inp["conv_ffn_w"][l]).reshape(3, FC, 128).transpose(2, 1, 0)),
            "cb": np.ascontiguousarray(np.asarray(inp["conv_ffn_b"][l]).reshape(FC, 128).T),
            "wdn": np.asarray(inp["w_ffn_down"][l])}


def _run(nc, ims):
    return run_bass_kernel_spmd(nc, ims, core_ids=list(range(NCORES))).results


def kernel(**inputs):
    inp = {k: np.asarray(v) for k, v in inputs.items()}
    x = np.ascontiguousarray(inp["x"][0].astype(np.float32))
    cores = range(NCORES)
    for l in range(2):
        ares = _run(build_A(l), [a_inputs(inp, x, l, c) for c in cores])
        bres = _run(build_B(), [b_inputs(inp, x, l, c, ares) for c in cores])
        xmid = np.ascontiguousarray(np.concatenate([np.asarray(bres[c]["xmT"]).T for c in cores], 0))
        cres = _run(build_C(), [c_inputs(inp, xmid, l, c) for c in cores])
        x = np.ascontiguousarray(np.concatenate([np.asarray(cres[c]["xoT"]).T for c in cores], 0))
    return x[None].astype(np.float32)
```
